# Optimizing a Trainium2 kernel written in Bass

```python
import jax, jax.numpy as jnp
from jax import lax
import numpy as np

D_MODEL = 2048
BATCH = 4
SEQ = 4096
DEPTH = 1
DEC_BATCH = 16
DEC_SEQ = 16
PAST_LEN = 4096

CHUNK = 64
MIX_DIM = D_MODEL
CONV_DIM = MIX_DIM // 2
HGRN_DIM = MIX_DIM - CONV_DIM
HGRN_HEADS = 8
HEAD_K = HGRN_DIM // HGRN_HEADS
HEAD_V = HGRN_DIM // HGRN_HEADS
CONV_W = 31
D_FF = 4 * D_MODEL
RBLK = CHUNK // 4
IN_COLS = 2 * CONV_DIM + 4 * HGRN_DIM
EPS = 1e-6

kernel_name = "hymba_conformer_hgrn2_stream_step"


def rms_norm(x, g):
    xf = x.astype(jnp.float32)
    y = xf * lax.rsqrt(jnp.mean(xf * xf, axis=-1, keepdims=True) + EPS)
    return (y * g.astype(jnp.float32)).astype(x.dtype)


def conformer_conv(u_in, buf, w_dw, b_dw, ln_g, ln_b):
    a, gate = jnp.split(u_in, 2, axis=-1)
    u = a * jax.nn.sigmoid(gate)
    full = jnp.concatenate([buf.astype(u.dtype), u], axis=1)
    c = lax.conv_general_dilated(
        full, w_dw[:, None, :].astype(u.dtype), (1,), 'VALID',
        dimension_numbers=('NWC', 'WIO', 'NWC'),
        feature_group_count=CONV_DIM) + b_dw.astype(u.dtype)
    cf = c.astype(jnp.float32)
    mu = jnp.mean(cf, axis=-1, keepdims=True)
    var = jnp.mean(jnp.square(cf - mu), axis=-1, keepdims=True)
    cn = (cf - mu) * lax.rsqrt(var + EPS) * ln_g.astype(jnp.float32) + ln_b.astype(jnp.float32)
    out = jax.nn.silu(cn).astype(u.dtype)
    return out, full[:, -(CONV_W - 1):]


def hgrn2(q_pre, f_pre, i_in, g_pre, lb, S0, norm_g):
    B, T, _ = q_pre.shape
    f32 = jnp.float32
    lbf = lb.astype(f32)
    q = jax.nn.silu(q_pre.astype(f32)).reshape(B, T, HGRN_HEADS, HEAD_K)
    f = lbf + (1.0 - lbf) * jax.nn.sigmoid(f_pre.astype(f32))
    logf = jnp.log(f).reshape(B, T, HGRN_HEADS, HEAD_K)
    k = (1.0 - f).reshape(B, T, HGRN_HEADS, HEAD_K)
    v = i_in.astype(f32).reshape(B, T, HGRN_HEADS, HEAD_V)
    pad = (-T) % RBLK
    n_blk = (T + pad) // RBLK

    def blocks(z):
        z = jnp.pad(z, ((0, 0), (0, pad), (0, 0), (0, 0)))
        return z.reshape(B, n_blk, RBLK, HGRN_HEADS, z.shape[-1]).transpose(1, 0, 3, 2, 4)

    qb, kb, vb, lfb = blocks(q), blocks(k), blocks(v), blocks(logf)
    bcum = jnp.cumsum(lfb, axis=3)
    bend = bcum[:, :, :, -1:, :]
    qd = qb * jnp.exp(bcum)
    kd = kb * jnp.exp(-bcum)
    kend = kb * jnp.exp(bend - bcum)
    gend = jnp.exp(bend[:, :, :, 0, :])
    mask = jnp.tril(jnp.ones((RBLK, RBLK), dtype=bool))
    att = jnp.where(mask, jnp.einsum('nbhld,nbhmd->nbhlm', qd, kd), 0.0)
    o_intra = jnp.einsum('nbhlm,nbhme->nbhle', att, vb)

    def step(S, xs):
        qd_j, kend_j, v_j, g_j = xs
        o_j = jnp.einsum('bhld,bhde->bhle', qd_j, S)
        S = g_j[..., None] * S + jnp.einsum('bhld,bhle->bhde', kend_j, v_j)
        return S, o_j

    S_T, o_inter = lax.scan(step, S0.astype(f32), (qd, kend, vb, gend))
    o = (o_intra + o_inter).transpose(1, 0, 3, 2, 4).reshape(B, n_blk * RBLK, HGRN_HEADS, HEAD_V)[:, :T]
    o = o * lax.rsqrt(jnp.mean(o * o, axis=-1, keepdims=True) + EPS)
    o = o.reshape(B, T, HGRN_DIM) * norm_g.astype(f32) * jax.nn.silu(g_pre.astype(f32))
    return o.astype(q_pre.dtype), S_T.astype(S0.dtype)


def layer(x, conv_buf, S0, lb, g_mix, w_in, w_dw, b_dw, ln_g, ln_b, hgrn_g, w_out, g_mlp, w_up, w_down):
    n = rms_norm(x, g_mix)
    proj = n @ w_in
    conv_in = proj[..., :2 * CONV_DIM]
    q_pre, f_pre, i_in, g_pre = jnp.split(proj[..., 2 * CONV_DIM:], 4, axis=-1)
    a_out, new_buf = conformer_conv(conv_in, conv_buf, w_dw, b_dw, ln_g, ln_b)
    b_out, S_T = hgrn2(q_pre, f_pre, i_in, g_pre, lb, S0, hgrn_g)
    h = x + jnp.concatenate([a_out, b_out], axis=-1) @ w_out
    m = rms_norm(h, g_mlp)
    h = h + jnp.square(jax.nn.relu(m @ w_up)) @ w_down
    return h, new_buf, S_T


def setup_inputs(seed: int = 0) -> dict:
    key = jax.random.key(seed)
    ks = jax.random.split(key, 20)
    nrm = jax.random.normal
    f32 = jnp.float32
    return {
        "x_prompt": nrm(ks[0], (BATCH, SEQ, D_MODEL), f32),
        "x_sample": nrm(ks[1], (DEC_BATCH, DEC_SEQ, D_MODEL), f32),
        "state_conv": 0.5 * nrm(ks[2], (DEPTH, DEC_BATCH, CONV_W - 1, CONV_DIM), f32),
        "state_hgrn": 0.5 * nrm(ks[3], (DEPTH, DEC_BATCH, HGRN_HEADS, HEAD_K, HEAD_V), f32),
        "norm_mix_g": 1.0 + 0.02 * nrm(ks[4], (DEPTH, D_MODEL), f32),
        "w_in": nrm(ks[5], (DEPTH, D_MODEL, IN_COLS), f32) * D_MODEL ** -0.5,
        "w_dw": nrm(ks[6], (DEPTH, CONV_W, CONV_DIM), f32) * CONV_W ** -0.5,
        "b_dw": 0.02 * nrm(ks[7], (DEPTH, CONV_DIM), f32),
        "ln_conv_g": 1.0 + 0.02 * nrm(ks[8], (DEPTH, CONV_DIM), f32),
        "ln_conv_b": 0.02 * nrm(ks[9], (DEPTH, CONV_DIM), f32),
        "lb_logits": 0.1 * nrm(ks[10], (DEPTH + 1, HGRN_DIM), f32),
        "hgrn_norm_g": 1.0 + 0.02 * nrm(ks[11], (DEPTH, HGRN_DIM), f32),
        "w_out": nrm(ks[12], (DEPTH, MIX_DIM, D_MODEL), f32) * MIX_DIM ** -0.5,
        "norm_mlp_g": 1.0 + 0.02 * nrm(ks[13], (DEPTH, D_MODEL), f32),
        "w_up": nrm(ks[14], (DEPTH, D_MODEL, D_FF), f32) * D_MODEL ** -0.5,
        "w_down": nrm(ks[15], (DEPTH, D_FF, D_MODEL), f32) * D_FF ** -0.5,
        "norm_final_g": 1.0 + 0.02 * nrm(ks[16], (D_MODEL,), f32),
    }


def reference(x_prompt, x_sample, state_conv, state_hgrn, norm_mix_g, w_in, w_dw, b_dw, ln_conv_g, ln_conv_b,
              lb_logits, hgrn_norm_g, w_out, norm_mlp_g, w_up, w_down, norm_final_g):
    lb_all = jnp.cumsum(jax.nn.softmax(lb_logits.astype(jnp.float32), axis=0), axis=0)
    bp = x_prompt.shape[0]
    hp, hs = x_prompt, x_sample
    conv_p, hgrn_p, conv_s, hgrn_s = [], [], [], []
    for l in range(DEPTH):
        w = (lb_all[l], norm_mix_g[l], w_in[l], w_dw[l], b_dw[l], ln_conv_g[l], ln_conv_b[l],
             hgrn_norm_g[l], w_out[l], norm_mlp_g[l], w_up[l], w_down[l])
        zero_buf = jnp.zeros((bp, CONV_W - 1, CONV_DIM), hp.dtype)
        zero_S = jnp.zeros((bp, HGRN_HEADS, HEAD_K, HEAD_V), state_hgrn.dtype)
        hp, cbp, sp = layer(hp, zero_buf, zero_S, *w)
        hs, cbs, ss = layer(hs, state_conv[l], state_hgrn[l], *w)
        conv_p.append(cbp)
        hgrn_p.append(sp)
        conv_s.append(cbs)
        hgrn_s.append(ss)
    y_prompt = rms_norm(hp, norm_final_g)
    y_sample = rms_norm(hs, norm_final_g)
    return (y_prompt, y_sample, jnp.stack(conv_p), jnp.stack(hgrn_p), jnp.stack(conv_s), jnp.stack(hgrn_s))
```

```python
import contextlib
import numpy as np
import concourse.bass as bass
import concourse.mybir as mybir
from concourse.bass_utils import run_bass_kernel_spmd

F32 = mybir.dt.float32
BF16 = mybir.dt.bfloat16
AF = mybir.ActivationFunctionType
ALU = mybir.AluOpType

D = 2048
KT = 16
CONV = 1024
NH = 8
DFF = 8192
IN_COLS = 6144
CW = 31
EPS = 1e-6
TW = 512
NPV = 72 + 248


class Op:
    __slots__ = ("eng", "fn", "deps", "sem", "val", "know", "flag", "is_dma", "waits")


class Sched:
    ENGS = ("pe", "act", "dve", "pool", "sp")
    EPOCH = 8000

    def __init__(self, nc):
        self.nc = nc
        self.all = []
        self.last_w = {}
        self.readers = {}
        self.dma_last = {}
        self.dma_count = {}

    def capture(self):
        self._cap = []

    def end_capture(self):
        lst = self._cap
        self._cap = None
        return lst

    def replay(self, *lists):
        items = []
        for li, lst in enumerate(lists):
            for i, a in enumerate(lst):
                items.append(((i + 0.5) / len(lst), li, i, a))
        items.sort(key=lambda x: (x[0], x[1], x[2]))
        for _, _, _, a in items:
            self.op(*a)

    def op(self, eng, fn, reads=(), writes=(), dma=None, ndma=1):
        if getattr(self, "_cap", None) is not None:
            self._cap.append((eng, fn, tuple(reads), tuple(writes), dma, ndma))
            return None
        o = Op()
        o.eng = eng
        o.fn = fn
        o.deps = set()
        o.is_dma = dma is not None
        o.flag = False
        o.know = None
        for k in reads:
            w = self.last_w.get(k)
            if w is not None:
                o.deps.add(w)
            if isinstance(k, tuple) and k[0] == "big":
                for r in self.readers.get(k, ()):
                    if r.eng != eng:
                        o.deps.add(r)
        for k in writes:
            w = self.last_w.get(k)
            if w is not None:
                o.deps.add(w)
            for r in self.readers.get(k, ()):
                o.deps.add(r)
        for k in reads:
            self.readers.setdefault(k, []).append(o)
        for k in writes:
            self.last_w[k] = o
            self.readers[k] = []
        if o.is_dma:
            p = self.dma_last.get(dma)
            if p is not None:
                o.deps.add(p)
            self.dma_last[dma] = o
            c = self.dma_count.get(dma, 0) + ndma
            self.dma_count[dma] = c
            o.sem = ("dma", dma)
            o.val = 16 * c
            o.flag = True
        o.deps.discard(o)
        self.all.append(o)
        return o

    def emit(self):
        nc = self.nc
        for o in self.all:
            for d in o.deps:
                if o.eng == "pe" and d.eng == "pe" and not d.is_dma:
                    continue
                d.flag = True
        cnt = {e: 0 for e in self.ENGS}
        for o in self.all:
            if o.is_dma:
                continue
            if o.flag:
                c = cnt[o.eng]
                o.sem = ("eng", o.eng, c // self.EPOCH)
                o.val = (c % self.EPOCH) + 1
                cnt[o.eng] = c + 1
        know = {e: {} for e in self.ENGS}
        for o in self.all:
            kn = know[o.eng]
            wm = {}
            dd = [d for d in o.deps if not (o.eng == "pe" and d.eng == "pe" and not d.is_dma)]
            for d in sorted(dd, key=lambda d: (str(d.sem), d.val)):
                if kn.get(d.sem, 0) < d.val:
                    wm[d.sem] = max(wm.get(d.sem, 0), d.val)
                    kn[d.sem] = d.val
                    if d.know:
                        for s, v in d.know.items():
                            if kn.get(s, 0) < v:
                                kn[s] = v
            o.waits = list(wm.items())
            if o.flag:
                o.know = dict(kn)
        final = [(("dma", k), 16 * c) for k, c in self.dma_count.items()]
        keys = []
        seen = set()
        for o in self.all:
            if o.flag and o.sem not in seen:
                seen.add(o.sem)
                keys.append(o.sem)
        self.nsem = len(keys)
        with contextlib.ExitStack() as st:
            sems = {}
            for i, k in enumerate(keys):
                sems[k] = st.enter_context(nc.semaphore("s%d" % i))
            block = st.enter_context(nc.Block())
            engobj = {"pe": "tensor", "act": "scalar", "dve": "vector", "pool": "gpsimd", "sp": "sync"}
            for e in self.ENGS:
                ops = [o for o in self.all if o.eng == e]
                fin = final if e == "sp" else []
                if not ops and not fin:
                    continue

                def body(eng, ops=ops, fin=fin):
                    for o in ops:
                        for s, v in o.waits:
                            eng.wait_ge(sems[s], v)
                        ins = o.fn(eng)
                        if o.flag:
                            if o.is_dma:
                                for i_ in ins:
                                    i_.then_inc(sems[o.sem], 16)
                            else:
                                ins.then_inc(sems[o.sem], 1)
                    for s, v in fin:
                        eng.wait_ge(sems[s], v)

                getattr(block, engobj[e])(body)


def build(n_prev=4, n_main=4, sample=True, ring_slots=3):
    nc = bass.Bass("TRN2", target_bir_lowering=False)
    dt_in = lambda n, sh: nc.dram_tensor(n, sh, F32, kind="ExternalInput").ap()
    dt_out = lambda n, sh: nc.dram_tensor(n, sh, F32, kind="ExternalOutput").ap()
    xm = dt_in("xm", [n_main * TW, D])
    xp = dt_in("xp", [max(n_prev, 1) * TW, D])
    xs = dt_in("xs", [32, D])
    sc = dt_in("sc", [2, 30, CONV])
    sh = dt_in("sh", [2, NH, 128, 128])
    w_in = dt_in("w_in", [D, IN_COLS])
    w_in_hm = dt_in("w_in_hm", [D, IN_COLS])
    w_out = dt_in("w_out", [D, D])
    w_up = dt_in("w_up", [D, DFF])
    w_down = dt_in("w_down", [DFF, D])
    pvec_d = dt_in("pvec", [128, NPV])
    gfin_d = dt_in("gfin", [D])
    hg_d = dt_in("hg", [CONV])
    yp = dt_out("yp", [n_main * TW, D])
    ys = dt_out("ys", [32, D])
    ncp = dt_out("ncp", [30, CONV])
    nhp = dt_out("nhp", [NH, 128, 128])
    ncs = dt_out("ncs", [2, 30, CONV])
    nhs = dt_out("nhs", [2, NH, 128, 128])
    hs_scr = nc.dram_tensor("hs_scr", [32, D], F32, kind="Internal").ap()

    w_in_v = w_in.rearrange("(kt p) c -> p kt c", p=128)
    w_hm_v = w_in_hm.rearrange("(kt p) c -> p kt c", p=128)
    w_out_v = w_out.rearrange("(kt p) c -> p kt c", p=128)
    w_up_v = w_up.rearrange("(kt p) c -> p kt c", p=128)
    w_down_v = w_down.rearrange("(kt p) c -> p kt c", p=128)

    st = contextlib.ExitStack()
    with st:
        sb = lambda n, shp, dt: st.enter_context(nc.sbuf_tensor(n, shp, dt))
        ps = lambda n, shp, dt: st.enter_context(nc.psum_tensor(n, shp, dt))
        NS = ring_slots
        ring = [sb("ring%d" % i, [128, KT, 512], BF16) for i in range(NS)]
        hbuf = sb("hbuf", [128, 4, D], F32)
        xn = [sb("xn%d" % i, [128, D], BF16) for i in range(2)]
        actT = sb("actT", [128, KT, TW], BF16)
        tok1k = xn[1][0:32, :].bitcast(F32)
        mixT = sb("mixT", [128, KT, TW], BF16)
        gfin = sb("gfin_r", [128, D], F32)
        hgr = sb("hg_r", [128, CONV], F32)
        pvec = sb("pvec_s", [128, NPV], F32)
        ident = sb("ident", [128, 128], BF16)
        identf = sb("identf", [128, 128], F32)
        maskT = sb("maskT", [128, 128], F32)
        zeros = sb("zeros", [128, 128], F32)
        zeros4 = zeros[:, 0:1].to_broadcast([128, TW])
        ones = sb("ones", [128, 128], BF16)
        lbt = sb("lbt", [128, 4, NH], F32)
        dg = sb("dg", [128, CW, 128], BF16)
        ubuf = sb("ubuf", [128, 8, 30 + TW], BF16)
        ubs = sb("ubs", [128, 8, 2, 46], BF16)
        scr = sb("scr", [128, 8, 512], F32)
        cbf = [sb("cbf%d" % i, [128, TW], BF16) for i in range(2)]
        csq = [sb("csq%d" % i, [128, TW], BF16) for i in range(2)]
        cmu = sb("cmu", [128, TW], F32)
        crs = sb("crs", [128, TW], F32)
        tmpE2 = [sb("tmpE%d" % i, [128, TW], F32) for i in range(2)]
        utail = sb("utail", [128, 8, 32], F32)
        Sst = [sb("S%d" % i, [128, NH, 128], F32) for i in range(2)]
        Sbf = [sb("Sbf%d" % i, [128, NH, 128], BF16) for i in range(2)]
        vtok = [sb("vtok%d" % i, [128, 128], BF16) for i in range(2)]
        ktok = [sb("ktok%d" % i, [128, 128], BF16) for i in range(2)]
        attT = [sb("attT%d" % i, [128, 128], BF16) for i in range(2)]
        btok = [sb("btok%d" % i, [128, 128], BF16) for i in range(2)]
        gt = [sb("gt%d" % i, [128, 128], F32) for i in range(2)]
        jsm = sb("jsm", [128, 128], BF16)
        kq = [sb("kq%d" % i, [128, 3 * TW], BF16) for i in range(2)]
        actS = sb("actS", [128, KT, 32], BF16)
        hidS = sb("hidS", [128, KT, 32], BF16)
        pend = [sb("pend%d" % i, [128, 4], F32) for i in range(2)]
        sfx = [sb("sfx%d" % i, [128, 4], F32) for i in range(2)]
        stt = sb("stt", [128, 64], F32)

        big = [ps("bank%d" % i, [128, 512], F32) for i in range(8)]
        bbf = lambda b_: big[b_][:, :].bitcast(BF16)
        cur = {"big": [0, 1, 2], "i": 0}

        def nbig():
            lst = cur["big"]
            v = lst[cur["i"] % len(lst)]
            cur["i"] += 1
            return v
        TN = [3, 4]
        TH = [5, 6]
        IGB = [3, 4]
        HB = [2, 7]

        S = Sched(nc)
        cnt = {"ring": 0, "big": 0, "pT": 0, "alt": 0, "ig": 0, "xn": 0, "cb": 0, "stt": 0, "cb2": 0}

        def nxt(name, mod):
            v = cnt[name]
            cnt[name] = (v + 1) % mod if mod else v + 1
            return v

        gmixT = lambda k: pvec[:, k:k + 1]
        gmlpT = lambda k: pvec[:, 16 + k:17 + k]
        bdwT = lambda ct: pvec[:, 32 + ct:33 + ct]
        lngT = lambda ct: pvec[:, 40 + ct:41 + ct]
        lnbT = lambda ct: pvec[:, 48 + ct:49 + ct]
        wdwT = lambda ct, j: pvec[:, 72 + ct * CW + j:73 + ct * CW + j]
        lbv = lambda h: lbt[:, 1, h:h + 1]
        omlv = lambda h: lbt[:, 2, h:h + 1]
        nomlv = lambda h: lbt[:, 3, h:h + 1]

        S.op("sp", lambda q: [q.dma_start(out=pvec[:], in_=pvec_d)], writes=["pvec"], dma="pvec")
        S.op("sp", lambda q: [q.dma_start(out=gfin[:], in_=gfin_d.partition_broadcast(128))], writes=["gfin"], dma="gfin")
        S.op("sp", lambda q: [q.dma_start(out=hgr[:], in_=hg_d.partition_broadcast(128))], writes=["hgr"], dma="hgr")
        S.op("pool", lambda g: g.memset(identf[:], 1.0), writes=["identf"])
        S.op("pool", lambda g: g.affine_select(out=identf[:], in_=identf[:], pattern=[[1, 128]], compare_op=ALU.is_equal,
                                               fill=0.0, base=0, channel_multiplier=-1), reads=["identf"], writes=["identf"])
        S.op("pool", lambda g: g.memset(maskT[:], 1.0), writes=["maskT"])
        S.op("pool", lambda g: g.affine_select(out=maskT[:], in_=maskT[:], pattern=[[1, 128]], compare_op=ALU.is_ge,
                                               fill=0.0, base=0, channel_multiplier=-1), reads=["maskT"], writes=["maskT"])
        S.op("dve", lambda v: v.tensor_copy(out=ident[:], in_=identf[:]), reads=["identf"], writes=["ident"])
        S.op("dve", lambda v: v.memset(zeros[:], 0.0), writes=["zeros"])

        S.op("dve", lambda v: v.memset(ones[:], 1.0), writes=["ones"])
        S.op("dve", lambda v: v.memset(ubuf[:, :, 0:30], 0.0), writes=[("u", ct) for ct in range(8)])
        S.op("dve", lambda v: v.tensor_sub(out=lbt[:, 0, :], in0=pvec[:, 64:72], in1=pvec[:, 56:64]), reads=["pvec"], writes=["lb0"])
        S.op("act", lambda a: a.activation(out=lbt[:, 0, :], in_=lbt[:, 0, :], func=AF.Exp), reads=["lb0"], writes=["lb0"])
        S.op("dve", lambda v: v.tensor_scalar_add(out=lbt[:, 1, :], in0=lbt[:, 0, :], scalar1=1.0), reads=["lb0"], writes=["lb1"])
        S.op("dve", lambda v: v.reciprocal(out=lbt[:, 1, :], in_=lbt[:, 1, :]), reads=["lb1"], writes=["lb1"])
        S.op("dve", lambda v: v.tensor_mul(out=lbt[:, 2, :], in0=lbt[:, 0, :], in1=lbt[:, 1, :]), reads=["lb0", "lb1"], writes=["lb2"])
        S.op("dve", lambda v: v.tensor_scalar_mul(out=lbt[:, 3, :], in0=lbt[:, 2, :], scalar1=-1.0), reads=["lb2"], writes=["lb3"])
        LBK = ["lb1", "lb2", "lb3"]
        nln = sb("nln", [128, 16], F32)
        S.op("dve", lambda v: v.tensor_scalar_mul(out=nln[:], in0=pvec[:, 40:56], scalar1=-1.0), reads=["pvec"], writes=["nln"])
        nlngT = lambda ct: nln[:, ct:ct + 1]
        nlnbT = lambda ct: nln[:, 8 + ct:9 + ct]
        PH0 = [("pH", 0), ("pHo", 0), ("pHs", 0)]
        PH1 = [("pH", 1), ("pHo", 1), ("pHs", 1)]

        def sk(g):
            return [("scr", g)]

        def ring_load(pieces):
            s = nxt("ring", NS)

            def fn(g, s=s, pieces=pieces):
                out = []
                for (view, kt0, c0, n, sc0) in pieces:
                    out.append(g.dma_start(out=ring[s][:, :, sc0:sc0 + n], in_=view[:, kt0:kt0 + KT, c0:c0 + n]))
                return out
            S.op("pool", fn, writes=[("ring", s)], dma=("ring", s), ndma=len(pieces))
            return s

        def gemm_fm(s, c0, ncols, acols, psap, pskey, akeys, src_=None):
            a0, a1 = acols
            src_ = actT if src_ is None else src_

            def fn(t):
                last = None
                for k in range(KT):
                    last = t.matmul(psap, lhsT=ring[s][:, k, c0:c0 + ncols], rhs=src_[:, k, a0:a1], start=(k == 0), stop=(k == KT - 1))
                return last
            S.op("pe", fn, reads=[("ring", s)] + akeys, writes=[pskey])

        def gemm_tm(src, srckeys, s, c0, ncols, off, n, psap, pskey):
            def fn(t):
                last = None
                for k in range(KT):
                    last = t.matmul(psap, lhsT=src[:, k, off:off + n], rhs=ring[s][:, k, c0:c0 + ncols], start=(k == 0), stop=(k == KT - 1))
                return last
            S.op("pe", fn, reads=[("ring", s)] + srckeys, writes=[pskey])

        def rsqrt_stat(col, n, scale):
            S.op("act", lambda a: a.activation(out=stt[:n, col + 1:col + 2], in_=stt[:n, col:col + 1], func=AF.Ln, scale=scale, bias=EPS),
                 reads=[("stt", col)], writes=[("stt", col + 1)])
            S.op("act", lambda a: a.activation(out=stt[:n, col + 1:col + 2], in_=stt[:n, col + 1:col + 2], func=AF.Exp, scale=-0.5),
                 reads=[("stt", col + 1)], writes=[("stt", col + 1)])

        def sigmoid_from(psap, pskey, outap, outkey, shape_n):
            okl = outkey if isinstance(outkey, list) else [outkey]
            S.op("act", lambda a: a.activation(out=outap, in_=psap, func=AF.Exp, scale=-1.0), reads=[pskey], writes=okl)
            S.op("act", lambda a: a.activation(out=outap, in_=outap, func=AF.Ln, bias=1.0), reads=okl, writes=okl)
            S.op("act", lambda a: a.activation(out=outap, in_=outap, func=AF.Exp, scale=-1.0), reads=okl, writes=okl)

        def norm_to_actT(mtiles, gfun, only=None):
            for mi, (off, n) in enumerate(mtiles):
                if only is not None and mi != only:
                    continue
                hk = [("h", mi, cb) for cb in range(4)]
                xi = nxt("xn", 2)
                sc_ = 2 * nxt("stt", 32)
                S.op("act", lambda a, mi=mi, n=n, xi=xi, sc_=sc_: a.activation(out=xn[xi][:n, :], in_=hbuf[:n, mi, :], func=AF.Square,
                                                                         accum_out=stt[:n, sc_:sc_ + 1]),
                     reads=hk, writes=[("xn", xi), ("stt", sc_)])
                rsqrt_stat(sc_, n, 1.0 / D)
                S.op("dve", lambda v, mi=mi, n=n, xi=xi, sc_=sc_: v.tensor_scalar_mul(out=xn[xi][:n, :], in0=hbuf[:n, mi, :],
                                                                               scalar1=stt[:n, sc_ + 1:sc_ + 2]),
                     reads=hk + [("stt", sc_ + 1)], writes=[("xn", xi)])
                for kg in range(4):
                    hf = TN[nxt("pT", 2)]

                    def tfn(t, n=n, xi=xi, kg=kg, hf=hf):
                        last = None
                        for j in range(4):
                            k = kg * 4 + j
                            last = t.transpose(out=bbf(hf)[:, j * 128:j * 128 + n], in_=xn[xi][:n, k * 128:(k + 1) * 128], identity=ident[:n, :n])
                        return last
                    S.op("pe", tfn, reads=[("xn", xi), "ident"], writes=[("big", hf)])
                    for j in range(4):
                        k = kg * 4 + j
                        if j % 2 == 0:
                            S.op("dve", lambda v, n=n, off=off, k=k, j=j, hf=hf: v.tensor_scalar_mul(
                                out=actT[:, k, off:off + n], in0=bbf(hf)[:, j * 128:j * 128 + n], scalar1=gfun(k)),
                                reads=[("big", hf), "pvec"], writes=[("actT", mi, k)])
                        else:
                            S.op("act", lambda a, n=n, off=off, k=k, j=j, hf=hf: a.activation(
                                out=actT[:, k, off:off + n], in_=bbf(hf)[:, j * 128:j * 128 + n], func=AF.Identity, scale=gfun(k)),
                                reads=[("big", hf), "pvec"], writes=[("actT", mi, k)])

        def process_tb(mode, xsrc, T, mtiles, sidx, out_y=None, last_main=False, part_a=False, extra=False):
            nm = len(mtiles)
            akeys = [("actT", mi, k) for mi in range(nm) for k in range(KT)]
            full = mode in ("main", "sample")
            for mi, (off, n) in enumerate(mtiles):
                S.op("sp", lambda q, mi=mi, off=off, n=n: [q.dma_start(out=hbuf[:n, mi, :], in_=xsrc[off:off + n, :])],
                     writes=[("h", mi, cb) for cb in range(4)], dma=("h", mi))
            norm_to_actT(mtiles, gmixT)

            cur["big"] = [0, 1, 2]
            if full or mode == "prev_last":
                cur["big"] = [0, 1, 2, 3, 4, 5] if full else [0, 1, 2]
                if mode == "prev_last":
                    ccols = (T - 32, T)
                else:
                    ccols = (0, T)
                cn = ccols[1] - ccols[0]
                slots = {}

                def glu(ct):
                    p, jj = divmod(ct, 2)
                    if jj == 0:
                        slots[p] = ring_load([(w_hm_v, 0, p * 512, 512, 0)])
                    s = slots[p]
                    ba = nbig()
                    gemm_fm(s, jj * 128, 128, ccols, big[ba][:, :cn], ("big", ba), akeys)
                    bg = nbig()
                    gemm_fm(s, 256 + jj * 128, 128, ccols, big[bg][:, :cn], ("big", bg), akeys)
                    tE = scr[:, ct, :]
                    tk = sk(ct)
                    sigmoid_from(big[bg][:, :cn], ("big", bg), tE[:, :cn], tk, cn)
                    if mode == "prev_last":
                        S.op("dve", lambda v: v.tensor_mul(out=ubuf[:, ct, 0:30], in0=big[ba][:, 2:32], in1=tE[:, 2:32]),
                             reads=[("big", ba)] + tk, writes=[("u", ct)])
                    elif mode == "main":
                        S.op("dve", lambda v: v.tensor_mul(out=ubuf[:, ct, 30:30 + T], in0=big[ba][:, :T], in1=tE[:, :T]),
                             reads=[("big", ba)] + tk, writes=[("u", ct)])
                        if last_main:
                            S.op("dve", lambda v: v.tensor_mul(out=utail[:, ct, :], in0=big[ba][:, T - 32:T], in1=tE[:, T - 32:T]),
                                 reads=[("big", ba)] + tk, writes=[("utail", ct)])
                    else:
                        for si in range(2):
                            S.op("dve", lambda v, si=si: v.tensor_mul(out=ubs[:, ct, si, 30:46], in0=big[ba][:, si * 16:si * 16 + 16],
                                                                     in1=tE[:, si * 16:si * 16 + 16]),
                                 reads=[("big", ba)] + tk, writes=[("us", ct, si)])
                        S.op("dve", lambda v: v.tensor_mul(out=utail[:, ct, :], in0=big[ba][:, 0:32], in1=tE[:, 0:32]),
                             reads=[("big", ba)] + tk, writes=[("utail", ct)])

                def conv_diag(ct):
                    S.op("dve", lambda v: v.tensor_tensor(out=dg[:], in0=identf[:].unsqueeze(1).to_broadcast([128, CW, 128]),
                                                          in1=pvec[:, 72 + ct * CW:72 + (ct + 1) * CW].unsqueeze(2).to_broadcast([128, CW, 128]),
                                                          op=ALU.mult),
                         reads=["identf", "pvec"], writes=[("dg", j) for j in range(CW)])

                def conv_rest(ct):
                    bc = nbig()

                    def cfn(t):
                        last = None
                        if mode == "main":
                            for j in range(CW):
                                last = t.matmul(big[bc][:, :T], lhsT=dg[:, j, :], rhs=ubuf[:, ct, j:j + T], start=(j == 0), stop=(j == CW - 1))
                        else:
                            for si in range(2):
                                for j in range(CW):
                                    last = t.matmul(big[bc][:, si * 16:si * 16 + 16], lhsT=dg[:, j, :], rhs=ubs[:, ct, si, j:j + 16],
                                                    start=(j == 0), stop=(j == CW - 1))
                        return last
                    ukeys = [("u", ct)] if mode == "main" else [("us", ct, 0), ("us", ct, 1)]
                    S.op("pe", cfn, reads=ukeys + [("dg", j) for j in range(CW)], writes=[("big", bc)])
                    if mode == "main":
                        S.op("dve", lambda v: v.tensor_copy(out=ubuf[:, ct, 0:30], in_=ubuf[:, ct, T:T + 30]),
                             reads=[("u", ct)], writes=[("u", ct)])
                    S.op("act", lambda a_: a_.activation(out=scr[:, ct, :T], in_=big[bc][:, :T], func=AF.Identity, bias=bdwT(ct)),
                         reads=[("big", bc), "pvec"], writes=sk(ct))
                    ci = nxt("cb", 2)
                    S.op("act", lambda a_: a_.activation(out=csq[ci][:, :T], in_=big[bc][:, :T], func=AF.Square, bias=bdwT(ct)),
                         reads=[("big", bc), "pvec"], writes=[("csq", ci)])
                    S.op("dve", lambda v: v.tensor_copy(out=cbf[ci][:, :T], in_=scr[:, ct, :T]),
                         reads=sk(ct), writes=[("cbf", ci)])
                    return ci

                def conv_stats(ct, ci):
                    S.op("pe", lambda t: t.matmul(big[6][:, :T], lhsT=ones[:], rhs=cbf[ci][:, :T], start=(ct == 0), stop=(ct == 7)),
                         reads=[("cbf", ci), "ones"], writes=[("big", 6)])
                    S.op("pe", lambda t: t.matmul(big[7][:, :T], lhsT=ones[:], rhs=csq[ci][:, :T], start=(ct == 0), stop=(ct == 7)),
                         reads=[("csq", ci), "ones"], writes=[("big", 7)])

                glu(0)
                pend_stats = None
                for ct in range(8):
                    if mode != "prev_last":
                        conv_diag(ct)
                    if ct + 1 < 8:
                        glu(ct + 1)
                    if mode != "prev_last":
                        ci_ = conv_rest(ct)
                        if pend_stats is not None:
                            conv_stats(*pend_stats)
                        pend_stats = (ct, ci_)
                if pend_stats is not None:
                    conv_stats(*pend_stats)
                if full:
                    pXsq = big[7]
                    S.op("act", lambda a: a.activation(out=cmu[:, :T], in_=big[6][:, :T], func=AF.Copy, scale=1.0 / CONV),
                         reads=[("big", 6)], writes=["cmu"])
                    S.op("dve", lambda v: v.tensor_mul(out=crs[:, :T], in0=cmu[:, :T], in1=cmu[:, :T]), reads=["cmu"], writes=["crs"])
                    S.op("dve", lambda v: v.scalar_tensor_tensor(out=crs[:, :T], in0=pXsq[:, :T], scalar=1.0 / CONV, in1=crs[:, :T],
                                                                 op0=ALU.mult, op1=ALU.subtract),
                         reads=[("big", 7), "crs"], writes=["crs"])
                    S.op("act", lambda a: a.activation(out=crs[:, :T], in_=crs[:, :T], func=AF.Ln, bias=EPS), reads=["crs"], writes=["crs"])
                    S.op("act", lambda a: a.activation(out=crs[:, :T], in_=crs[:, :T], func=AF.Exp, scale=-0.5), reads=["crs"], writes=["crs"])
                    def ln_ct(ct):
                        tE_ = tmpE2[ct % 2]
                        tEk = ("tmpE", ct % 2)
                        S.op("dve", lambda v, ct=ct: v.tensor_sub(out=scr[:, ct, :T], in0=scr[:, ct, :T], in1=cmu[:, :T]),
                             reads=sk(ct) + ["cmu"], writes=sk(ct))
                        S.op("dve", lambda v, ct=ct: v.tensor_mul(out=scr[:, ct, :T], in0=scr[:, ct, :T], in1=crs[:, :T]),
                             reads=sk(ct) + ["crs"], writes=sk(ct))
                        S.op("act", lambda a, ct=ct, tE_=tE_: a.activation(out=tE_[:, :T], in_=scr[:, ct, :T], func=AF.Exp, scale=nlngT(ct), bias=nlnbT(ct)),
                             reads=sk(ct) + ["nln"], writes=[tEk])
                        S.op("act", lambda a, ct=ct: a.activation(out=scr[:, ct, :T], in_=scr[:, ct, :T], func=AF.Identity, scale=lngT(ct), bias=lnbT(ct)),
                             reads=sk(ct) + ["pvec"], writes=sk(ct))
                        S.op("act", lambda a, tE_=tE_: a.activation(out=tE_[:, :T], in_=tE_[:, :T], func=AF.Ln, bias=1.0), reads=[tEk], writes=[tEk])
                        S.op("act", lambda a, tE_=tE_: a.activation(out=tE_[:, :T], in_=tE_[:, :T], func=AF.Exp, scale=-1.0), reads=[tEk], writes=[tEk])
                        S.op("dve", lambda v, ct=ct, tE_=tE_: v.tensor_mul(out=mixT[:, ct, :T], in0=scr[:, ct, :T], in1=tE_[:, :T]),
                             reads=sk(ct) + [tEk], writes=[("mixT", ct)])

                    for ct in (4, 5, 6, 7):
                        ln_ct(ct)

            if not full:
                cur["big"] = [0, 1]
                X = lambda g: scr[:, g, :T]
                vall = mixT[:, :, :].rearrange("p a b -> p (a b)")
                kt4 = [scr[:, 6, :].bitcast(BF16), scr[:, 7, :].bitcast(BF16)]
                for half in range(2):
                    s_i = ring_load([(w_in_v, 0, 4096 + half * 512, 512, 0)])
                    for mi, (off, n) in enumerate(mtiles):
                        b = IGB[nxt("ig", 2)]
                        gemm_tm(actT, [("actT", mi, k) for k in range(KT)], s_i, 0, 512, off, n, big[b][:n, :], ("big", b))
                        S.op("act", lambda a, b=b, n=n, mi=mi, half=half: a.activation(
                            out=vall[:n, mi * 1024 + half * 512:mi * 1024 + half * 512 + 512], in_=big[b][:n, :], func=AF.Copy),
                            reads=[("big", b)], writes=[("mixT", mi * 2 + half)])
                fs = {}

                def fgemm(h):
                    half, j = divmod(h, 4)
                    if j == 0:
                        fs[half] = ring_load([(w_in_v, 0, 3072 + half * 512, 512, 0)])
                    bf_ = nbig()
                    gemm_fm(fs[half], j * 128, 128, (0, T), big[bf_][:, :T], ("big", bf_), akeys)
                    return bf_

                def pgates(h, bf_):
                    hp = h % 2
                    KE = kq[hp][:, TW:TW + T]
                    kek = ("KE", hp)
                    sigmoid_from(big[bf_][:, :T], ("big", bf_), X(0), ("scr", 0), T)
                    S.op("act", lambda a: a.activation(out=X(1), in_=X(0), func=AF.Ln, scale=omlv(h), bias=lbv(h)),
                         reads=[("scr", 0)] + LBK, writes=[("scr", 1)])
                    S.op("dve", lambda v: v.tensor_scalar(out=X(2), in0=X(0), scalar1=nomlv(h), scalar2=omlv(h), op0=ALU.mult, op1=ALU.add),
                         reads=[("scr", 0)] + LBK, writes=[("scr", 2)])
                    S.op("dve", lambda v: v.tensor_tensor_scan(out=X(3), data0=X(1), data1=zeros[:, 0:1].to_broadcast([128, T]), initial=0.0, op0=ALU.add, op1=ALU.add),
                         reads=[("scr", 1), "zeros"], writes=[("scr", 3)])
                    S.op("act", lambda a: a.activation(out=X(4), in_=X(3), func=AF.Exp, scale=-1.0, bias=scr[:, 3, T - 1:T]),
                         reads=[("scr", 3)], writes=[("scr", 4)])
                    S.op("act", lambda a: a.activation(out=sfx[hp][:, 0:1], in_=scr[:, 3, T - 1:T], func=AF.Exp),
                         reads=[("scr", 3)], writes=[("sfx", hp)])
                    S.op("dve", lambda v: v.tensor_mul(out=KE, in0=X(2), in1=X(4)), reads=[("scr", 2), ("scr", 4)], writes=[kek])

                def pupdate(h):
                    hp = h % 2
                    KE = kq[hp][:, TW:TW + T]
                    kek = ("KE", hp)
                    al = nxt("alt", 2)
                    hb = HB[al]
                    hf = TH[nxt("pT", 2)]

                    def tfn(t):
                        last = None
                        for c, (off, n) in enumerate(mtiles):
                            last = t.transpose(out=bbf(hf)[:n, c * 128:(c + 1) * 128], in_=KE[:, off:off + n], identity=ident[:])
                        return last
                    S.op("pe", tfn, reads=[kek, "ident"], writes=[("big", hf)])
                    S.op("act", lambda a: a.activation(out=kt4[al][:, 0:512], in_=bbf(hf)[:, 0:512], func=AF.Copy),
                         reads=[("big", hf)], writes=sk(6 + al))

                    def sfn(t):
                        last = None
                        for c, (off, n) in enumerate(mtiles):
                            last = t.matmul(big[hb][:, 256:384], lhsT=kt4[al][:n, c * 128:(c + 1) * 128],
                                            rhs=vall[:n, c * 1024 + h * 128:c * 1024 + (h + 1) * 128], start=(c == 0), stop=(c == nm - 1))
                        return last
                    S.op("pe", sfn, reads=sk(6 + al) + [("mixT", c * 2 + h // 4) for c in range(nm)], writes=[("big", hb)])
                    S.op("dve", lambda v: v.scalar_tensor_tensor(
                        out=Sst[0][:, h, :], in0=Sst[0][:, h, :], scalar=sfx[hp][:, 0:1], in1=big[hb][:, 256:384],
                        op0=ALU.mult, op1=ALU.add),
                        reads=[("S", 0, h), ("sfx", hp), ("big", hb)], writes=[("S", 0, h)])
                    S.op("act", lambda a: a.activation(out=Sbf[0][:, h, :], in_=Sst[0][:, h, :], func=AF.Copy),
                         reads=[("S", 0, h)], writes=[("Sbf", 0, h)])

                fb = {0: fgemm(0)}
                pgates(0, fb[0])
                fb[1] = fgemm(1)
                for h in range(NH):
                    S.capture()
                    pupdate(h)
                    la = S.end_capture()
                    S.capture()
                    if h + 1 < NH:
                        pgates(h + 1, fb[h + 1])
                    if h + 2 < NH:
                        fb[h + 2] = fgemm(h + 2)
                    lb_ = S.end_capture()
                    if lb_:
                        S.replay(la, lb_)
                    else:
                        S.replay(la)
                return

            cur["big"] = [0, 1]
            X = lambda g: scr[:, g, :T]
            ncol = 256 if full else 128
            csz = mtiles[0][1]

            def head_gemm(h):
                s = ring_load([(w_hm_v, 0, 2048 + h * 512, 512, 0)])
                bf_ = nbig()
                gemm_fm(s, 128, 128, (0, T), big[bf_][:, :T], ("big", bf_), akeys)
                bq = None
                if full:
                    bq = nbig()
                    gemm_fm(s, 0, 128, (0, T), big[bq][:, :T], ("big", bq), akeys)
                return s, bf_, bq

            def head_gates(h, bf_, bq):
                hp = h % 2
                KD, KE, QD = kq[hp][:, 0:T], kq[hp][:, TW:TW + T], kq[hp][:, 2 * TW:2 * TW + T]
                kdk, kek, qdk = ("KD", hp), ("KE", hp), ("QD", hp)
                G = lambda i: scr[:, 4 + i, :T]
                gk = lambda i: ("scr", 4 + i)
                sigmoid_from(big[bf_][:, :T], ("big", bf_), G(0), gk(0), T)
                S.op("act", lambda a: a.activation(out=G(1), in_=G(0), func=AF.Ln, scale=omlv(h), bias=lbv(h)),
                     reads=[gk(0)] + LBK, writes=[gk(1)])
                S.op("dve", lambda v: v.tensor_scalar(out=G(2), in0=G(0), scalar1=nomlv(h), scalar2=omlv(h), op0=ALU.mult, op1=ALU.add),
                     reads=[gk(0)] + LBK, writes=[gk(2)])
                for mi, (off, n) in enumerate(mtiles):
                    S.op("dve", lambda v, off=off, n=n: v.tensor_tensor_scan(out=scr[:, 7, off:off + n], data0=scr[:, 5, off:off + n],
                                                                             data1=zeros[:, 0:n], initial=0.0, op0=ALU.add, op1=ALU.add),
                         reads=[gk(1), "zeros"], writes=[gk(3)])
                S.op("act", lambda a: a.activation(out=G(0), in_=G(3), func=AF.Exp, scale=-1.0), reads=[gk(3)], writes=[gk(0)])
                S.op("act", lambda a: a.activation(out=G(3), in_=G(3), func=AF.Exp), reads=[gk(3)], writes=[gk(3)])
                S.op("dve", lambda v: v.tensor_copy(out=pend[hp][:, 0:nm], in_=scr[:, 7, csz - 1:T:csz]), reads=[gk(3)], writes=[("pend", hp)])
                S.op("dve", lambda v: v.tensor_mul(out=KD, in0=G(2), in1=G(0)), reads=[gk(2), gk(0)], writes=[kdk])
                for mi, (off, n) in enumerate(mtiles):
                    S.op("act", lambda a, off=off, n=n, mi=mi: a.activation(out=KE[:, off:off + n], in_=KD[:, off:off + n], func=AF.Identity,
                                                                           scale=pend[hp][:, mi:mi + 1]),
                         reads=[kdk, ("pend", hp)], writes=[kek])
                if full:
                    sigmoid_from(big[bq][:, :T], ("big", bq), G(1), gk(1), T)
                    S.op("dve", lambda v: v.tensor_mul(out=G(1), in0=big[bq][:, :T], in1=G(1)), reads=[("big", bq), gk(1)], writes=[gk(1)])
                    S.op("dve", lambda v: v.tensor_mul(out=QD, in0=G(1), in1=G(3)), reads=[gk(1), gk(3)], writes=[qdk])

            def chunk_s1(h, s, mi):
                off, n = mtiles[mi]
                hp = h % 2
                KD, KE, QD = kq[hp][:, 0:T], kq[hp][:, TW:TW + T], kq[hp][:, 2 * TW:2 * TW + T]
                kdk, kek, qdk = ("KD", hp), ("KE", hp), ("QD", hp)
                al = nxt("alt", 2)
                hb = HB[al]
                ig = IGB[nxt("ig", 2)]
                gemm_tm(actT, [("actT", mi, k) for k in range(KT)], s, 256, ncol, off, n, big[ig][:n, 0:ncol], ("big", ig))
                S.op("dve", lambda v: v.tensor_copy(out=vtok[al][:n, :], in_=big[ig][:n, 0:128]),
                     reads=[("big", ig)], writes=[("vtok", al)])
                if full:
                    sigmoid_from(big[ig][:n, 128:256], ("big", ig), gt[al][:n, :], ("gt", al), n)
                    S.op("dve", lambda v: v.tensor_mul(out=gt[al][:n, :], in0=big[ig][:n, 128:256], in1=gt[al][:n, :]),
                         reads=[("big", ig), ("gt", al)], writes=[("gt", al)])
                    S.op("dve", lambda v: v.tensor_mul(out=gt[al][:n, :], in0=gt[al][:n, :], in1=hgr[:n, h * 128:(h + 1) * 128]),
                         reads=[("gt", al), "hgr"], writes=[("gt", al)])
                hf = TH[nxt("pT", 2)]
                S.op("pe", lambda t: t.transpose(out=bbf(hf)[:n, 0:128], in_=KE[:, off:off + n], identity=ident[:]),
                     reads=[kek, "ident"], writes=[("big", hf)])
                S.op("act", lambda a: a.activation(out=ktok[al][:n, :], in_=bbf(hf)[:n, 0:128], func=AF.Copy),
                     reads=[("big", hf)], writes=[("ktok", al)])
                if full:
                    S.op("pe", lambda t: t.matmul(big[hb][:n, 0:n], lhsT=KD[:, off:off + n], rhs=QD[:, off:off + n], start=True, stop=True),
                         reads=[kdk, qdk], writes=[("big", hb)])
                    S.op("dve", lambda v: v.tensor_mul(out=attT[al][:n, :n], in0=big[hb][:n, 0:n], in1=maskT[:n, :n]),
                         reads=[("big", hb), "maskT"], writes=[("attT", al)])
                return al, hb

            def chunk_s2(h, mi, al, hb):
                off, n = mtiles[mi]
                si = sidx[mi]
                hp = h % 2
                QD = kq[hp][:, 2 * TW:2 * TW + T]
                qdk = ("QD", hp)
                if full:
                    def ofn(t):
                        t.matmul(big[hb][:n, 128:256], lhsT=attT[al][:n, :n], rhs=vtok[al][:n, :], start=True, stop=False)
                        return t.matmul(big[hb][:n, 128:256], lhsT=QD[:, off:off + n], rhs=Sbf[si][:, h, :], start=False, stop=True)
                    S.op("pe", ofn, reads=[("attT", al), ("vtok", al), qdk, ("Sbf", si, h)], writes=[("big", hb)])
                S.op("pe", lambda t: t.matmul(big[hb][:, 256:384], lhsT=ktok[al][:n, :], rhs=vtok[al][:n, :], start=True, stop=True),
                     reads=[("ktok", al), ("vtok", al)], writes=[("big", hb)])
                S.op("dve", lambda v: v.scalar_tensor_tensor(
                    out=Sst[si][:, h, :], in0=Sst[si][:, h, :], scalar=pend[hp][:, mi:mi + 1], in1=big[hb][:, 256:384],
                    op0=ALU.mult, op1=ALU.add),
                    reads=[("S", si, h), ("pend", hp), ("big", hb)], writes=[("S", si, h)])
                S.op("act", lambda a: a.activation(out=Sbf[si][:, h, :], in_=Sst[si][:, h, :], func=AF.Copy),
                     reads=[("S", si, h)], writes=[("Sbf", si, h)])
                if full:
                    sc_ = 2 * nxt("stt", 32)
                    S.op("act", lambda a: a.activation(out=jsm[:n, :], in_=big[hb][:n, 128:256], func=AF.Square, accum_out=stt[:n, sc_:sc_ + 1]),
                         reads=[("big", hb)], writes=["jsm", ("stt", sc_)])
                    rsqrt_stat(sc_, n, 1.0 / 128)
                    S.op("dve", lambda v: v.scalar_tensor_tensor(
                        out=btok[al][:n, :], in0=big[hb][:n, 128:256], scalar=stt[:n, sc_ + 1:sc_ + 2], in1=gt[al][:n, :],
                        op0=ALU.mult, op1=ALU.mult),
                        reads=[("big", hb), ("stt", sc_ + 1), ("gt", al)], writes=[("btok", al)])

            def chunk_s2b(h, mi, al, hb):
                off, n = mtiles[mi]
                if full:
                    hf2 = TH[nxt("pT", 2)]
                    S.op("pe", lambda t: t.transpose(out=bbf(hf2)[:, 0:n], in_=btok[al][:n, :], identity=ident[:n, :n]),
                         reads=[("btok", al), "ident"], writes=[("big", hf2)])
                    S.op("dve", lambda v: v.tensor_copy(out=mixT[:, 8 + h, off:off + n], in_=bbf(hf2)[:, 0:n]),
                         reads=[("big", hf2)], writes=[("mixT", 8 + h)])

            def chunks(h, s):
                ctxs = {0: chunk_s1(h, s, 0)}
                for mi in range(nm):
                    if mi + 1 < nm:
                        ctxs[mi + 1] = chunk_s1(h, s, mi + 1)
                    if mi >= 1:
                        chunk_s2b(h, mi - 1, *ctxs[mi - 1])
                    chunk_s2(h, mi, *ctxs[mi])
                chunk_s2b(h, nm - 1, *ctxs[nm - 1])

            g = {0: head_gemm(0)}
            head_gates(0, g[0][1], g[0][2])
            if NH > 1:
                g[1] = head_gemm(1)
            for h in range(NH):
                S.capture()
                chunks(h, g[h][0])
                la = S.end_capture()
                S.capture()
                if h + 1 < NH:
                    head_gates(h + 1, g[h + 1][1], g[h + 1][2])
                if h + 2 < NH:
                    g[h + 2] = head_gemm(h + 2)
                lb_ = S.end_capture()
                lists = [la] + ([lb_] if lb_ else [])
                if h == 0:
                    S.capture()
                    for ct in (0, 1, 2, 3):
                        ln_ct(ct)
                    lists.append(S.end_capture())
                S.replay(*lists)
            if not full:
                return

            cur["big"] = [0, 1, 2, 3, 4, 5]
            mkeys = [("mixT", k) for k in range(KT)]
            for cb in range(4):
                s = ring_load([(w_out_v, 0, cb * 512, 512, 0)])
                for mi, (off, n) in enumerate(mtiles):
                    b = nbig()
                    gemm_tm(mixT, mkeys, s, 0, 512, off, n, big[b][:n, :], ("big", b))
                    S.op("dve", lambda v, mi=mi, n=n, cb=cb, b=b: v.tensor_add(out=hbuf[:n, mi, cb * 512:(cb + 1) * 512],
                                                                              in0=hbuf[:n, mi, cb * 512:(cb + 1) * 512], in1=big[b][:n, :]),
                         reads=[("h", mi, cb), ("big", b)], writes=[("h", mi, cb)])
                    if cb == 3:
                        norm_to_actT(mtiles, gmlpT, only=mi)
            if part_a:
                S.op("dve", lambda v: v.tensor_copy(out=actS[:, :, 0:T], in_=actT[:, :, 0:T]), reads=akeys, writes=["actS"])
                for mi, (off, n) in enumerate(mtiles):
                    S.op("sp", lambda q, mi=mi, off=off, n=n: [q.dma_start(out=hs_scr[off:off + n, :], in_=hbuf[:n, mi, :])],
                         reads=[("h", mi, cb) for cb in range(4)], writes=["hs_scr"], dma=("hss", mi))
                return
            def final_norm_store(mi, n, dst):
                hk = [("h", mi, cb) for cb in range(4)]
                xi = nxt("xn", 2)
                sc_ = 2 * nxt("stt", 32)
                S.op("act", lambda a: a.activation(out=xn[xi][:n, :], in_=hbuf[:n, mi, :], func=AF.Square, accum_out=stt[:n, sc_:sc_ + 1]),
                     reads=hk, writes=[("xn", xi), ("stt", sc_)])
                rsqrt_stat(sc_, n, 1.0 / D)
                S.op("dve", lambda v: v.scalar_tensor_tensor(out=hbuf[:n, mi, :], in0=hbuf[:n, mi, :], scalar=stt[:n, sc_ + 1:sc_ + 2],
                                                             in1=gfin[:n, :], op0=ALU.mult, op1=ALU.mult),
                     reads=hk + [("stt", sc_ + 1), "gfin"], writes=hk)
                S.op("sp", lambda q: [q.dma_start(out=dst, in_=hbuf[:n, mi, :])], reads=hk, dma=("yo", mi))

            if extra:
                cur["big"] = [0, 1, 2, 3]
            hkeys = [("hidS", k) for k in range(KT)]
            for qf in range(4):
                for sbk in range(4):
                    s = ring_load([(w_up_v, 0, qf * 2048 + sbk * 512, 512, 0)])
                    for j in range(4):
                        b = nbig()
                        gemm_fm(s, j * 128, 128, (0, T), big[b][:, :T], ("big", b), akeys)
                        ri = nxt("cb", 2)
                        S.op("act", lambda a, b=b, ri=ri: a.activation(out=scr[:, ri, :T], in_=big[b][:, :T], func=AF.Relu),
                             reads=[("big", b)], writes=[("scr", ri)])
                        S.op("dve", lambda v, ri=ri, sbk=sbk, j=j: v.tensor_mul(out=mixT[:, sbk * 4 + j, :T], in0=scr[:, ri, :T], in1=scr[:, ri, :T]),
                             reads=[("scr", ri)], writes=[("mixT", sbk * 4 + j)])
                        if extra:
                            b2 = nbig()
                            gemm_fm(s, j * 128, 128, (0, 32), big[b2][:, :32], ("big", b2), ["actS"], src_=actS)
                            r2 = 2 + nxt("cb2", 2)
                            S.op("act", lambda a, b2=b2, r2=r2: a.activation(out=scr[:, r2, :32], in_=big[b2][:, :32], func=AF.Relu),
                                 reads=[("big", b2)], writes=[("scr", r2)])
                            S.op("dve", lambda v, r2=r2, sbk=sbk, j=j: v.tensor_mul(out=hidS[:, sbk * 4 + j, :], in0=scr[:, r2, :32], in1=scr[:, r2, :32]),
                                 reads=[("scr", r2)], writes=[("hidS", sbk * 4 + j)])
                for cb in range(4):
                    s = ring_load([(w_down_v, qf * 16, cb * 512, 512, 0)])
                    for mi, (off, n) in enumerate(mtiles):
                        b = nbig()
                        gemm_tm(mixT, mkeys, s, 0, 512, off, n, big[b][:n, :], ("big", b))
                        S.op("dve", lambda v, mi=mi, n=n, cb=cb, b=b: v.tensor_add(out=hbuf[:n, mi, cb * 512:(cb + 1) * 512],
                                                                                  in0=hbuf[:n, mi, cb * 512:(cb + 1) * 512], in1=big[b][:n, :]),
                             reads=[("h", mi, cb), ("big", b)], writes=[("h", mi, cb)])
                        if qf == 3 and cb == 3:
                            final_norm_store(mi, n, out_y[off:off + n, :])
                    if extra:
                        def sfn(t, s=s, cb=cb, qf=qf):
                            last = None
                            for k in range(KT):
                                last = t.matmul(big[4 + cb][:32, :], lhsT=hidS[:, k, 0:32], rhs=ring[s][:, k, :],
                                                start=(qf == 0 and k == 0), stop=(qf == 3 and k == KT - 1))
                            return last
                        S.op("pe", sfn, reads=[("ring", s)] + hkeys, writes=[("big", 4 + cb)])

            if extra:
                S.op("sp", lambda q: [q.dma_start(out=hbuf[:32, 0, :], in_=hs_scr)], reads=["hs_scr"],
                     writes=[("h", 0, cb) for cb in range(4)], dma=("h", 0))
                for cb in range(4):
                    S.op("dve", lambda v, cb=cb: v.tensor_add(out=hbuf[:32, 0, cb * 512:(cb + 1) * 512], in0=hbuf[:32, 0, cb * 512:(cb + 1) * 512],
                                                             in1=big[4 + cb][:32, :]),
                         reads=[("h", 0, cb), ("big", 4 + cb)], writes=[("h", 0, cb)])
                final_norm_store(0, 32, ys)

        def store_conv_tail(nrows_src, src_lo):
            pflat = big[7]
            for ct in range(8):
                q4 = ct % 4
                S.op("pe", lambda t, ct=ct, q4=q4: t.transpose(out=pflat[:nrows_src, q4 * 128:(q4 + 1) * 128],
                                                               in_=utail[:, ct, src_lo:src_lo + nrows_src], identity=identf[:]),
                     reads=[("utail", ct), "identf"], writes=[("big", 7)])
                S.op("dve", lambda v, ct=ct, q4=q4: v.tensor_copy(out=tok1k[:nrows_src, ct * 128:(ct + 1) * 128],
                                                                 in_=pflat[:nrows_src, q4 * 128:(q4 + 1) * 128]),
                     reads=[("big", 7)], writes=[("tok1k", ct), ("xn", 1)])

        pm = [(i * 128, 128) for i in range(4)]
        if sample:
            for j in range(2):
                S.op("sp", lambda q, j=j: [q.dma_start(out=Sst[j][:], in_=sh[j].rearrange("h d e -> d h e"))],
                     writes=[("S", j, h) for h in range(NH)], dma=("shl", j))
                S.op("dve", lambda v, j=j: v.tensor_copy(out=Sbf[j][:], in_=Sst[j][:]), reads=[("S", j, h) for h in range(NH)],
                     writes=[("Sbf", j, h) for h in range(NH)])
                S.op("sp", lambda q, j=j: [q.dma_start(out=tok1k[0:30, :], in_=sc[j])], writes=[("tok1k", ct) for ct in range(8)] + [("xn", 1)], dma="scl")
                for ct in range(8):
                    S.op("pe", lambda t, ct=ct: t.transpose(out=big[6][:, ct * 32:ct * 32 + 30], in_=tok1k[0:30, ct * 128:(ct + 1) * 128], identity=identf[0:30, 0:30]),
                         reads=[("tok1k", ct), ("xn", 1), "identf"], writes=[("big", 6)])
                    S.op("dve", lambda v, ct=ct, j=j: v.tensor_copy(out=ubs[:, ct, j, 0:30], in_=big[6][:, ct * 32:ct * 32 + 30]),
                         reads=[("big", 6)], writes=[("us", ct, j)])
                S.op("sp", lambda q, j=j: [q.dma_start(out=ncs[j, 0:14, :], in_=sc[j, 16:30, :])], dma=("ncs0", j))
            process_tb("sample", xs, 32, [(0, 16), (16, 16)], [0, 1], part_a=True)
            for j in range(2):
                S.op("sp", lambda q, j=j: [q.dma_start(out=nhs[j].rearrange("h d e -> d h e"), in_=Sst[j][:])],
                     reads=[("S", j, h) for h in range(NH)], dma=("nhs", j))
                store_conv_tail(16, j * 16)
                S.op("sp", lambda q, j=j: [q.dma_start(out=ncs[j, 14:30, :], in_=tok1k[0:16, :])], reads=[("tok1k", ct) for ct in range(8)] + [("xn", 1)],
                     dma=("ncs1", j))
        S.op("dve", lambda v: v.memset(Sst[0][:], 0.0), writes=[("S", 0, h) for h in range(NH)])
        S.op("dve", lambda v: v.memset(Sbf[0][:], 0.0), writes=[("Sbf", 0, h) for h in range(NH)])
        for tb in range(n_prev):
            process_tb("prev_last" if tb == n_prev - 1 else "prev", xp[tb * TW:(tb + 1) * TW, :], TW, pm, [0, 0, 0, 0])
        for tb in range(n_main):
            last = tb == n_main - 1
            process_tb("main", xm[tb * TW:(tb + 1) * TW, :], TW, pm, [0, 0, 0, 0], out_y=yp[tb * TW:(tb + 1) * TW, :],
                       last_main=last, extra=(last and sample))
        S.op("sp", lambda q: [q.dma_start(out=nhp.rearrange("h d e -> d h e"), in_=Sst[0][:])], reads=[("S", 0, h) for h in range(NH)], dma="nhp")
        store_conv_tail(32, 0)
        S.op("sp", lambda q: [q.dma_start(out=ncp[:, :], in_=tok1k[2:32, :])], reads=[("tok1k", ct) for ct in range(8)] + [("xn", 1)], dma="ncp")
        S.emit()
    return nc


def _pack_pvec(norm_mix_g, norm_mlp_g, b_dw, ln_g, ln_b, lb_logits, w_dw):
    pv = np.zeros((128, NPV), np.float32)
    pv[:, 0:16] = norm_mix_g.reshape(16, 128).T
    pv[:, 16:32] = norm_mlp_g.reshape(16, 128).T
    pv[:, 32:40] = b_dw.reshape(8, 128).T
    pv[:, 40:48] = ln_g.reshape(8, 128).T
    pv[:, 48:56] = ln_b.reshape(8, 128).T
    pv[:, 56:64] = lb_logits[0].reshape(8, 128).T
    pv[:, 64:72] = lb_logits[1].reshape(8, 128).T
    pv[:, 72:72 + 248] = w_dw.T.reshape(8, 128, CW).transpose(1, 0, 2).reshape(128, 248)
    return pv


_NC_CACHE = {}


def kernel(x_prompt, x_sample, state_conv, state_hgrn, norm_mix_g, w_in, w_dw, b_dw, ln_conv_g, ln_conv_b,
           lb_logits, hgrn_norm_g, w_out, norm_mlp_g, w_up, w_down, norm_final_g):
    f = lambda a: np.ascontiguousarray(np.asarray(a, dtype=np.float32))
    x_prompt, x_sample, state_conv, state_hgrn = f(x_prompt), f(x_sample), f(state_conv), f(state_hgrn)
    B, SEQ, _ = x_prompt.shape
    half = SEQ // 2
    pv = _pack_pvec(f(norm_mix_g)[0], f(norm_mlp_g)[0], f(b_dw)[0], f(ln_conv_g)[0], f(ln_conv_b)[0], f(lb_logits), f(w_dw)[0])
    w_in0 = f(w_in)[0]
    cols = []
    for p in range(4):
        cols += list(range(p * 256, (p + 1) * 256)) + list(range(1024 + p * 256, 1024 + (p + 1) * 256))
    for h in range(NH):
        for i in range(4):
            cols += list(range(2048 + i * 1024 + h * 128, 2048 + i * 1024 + (h + 1) * 128))
    w_in_hm = np.ascontiguousarray(w_in0[:, np.asarray(cols)])
    shared = {"w_in": w_in0, "w_in_hm": w_in_hm, "w_out": f(w_out)[0], "w_up": f(w_up)[0], "w_down": f(w_down)[0], "pvec": pv,
              "gfin": f(norm_final_g), "hg": f(hgrn_norm_g)[0]}
    zeros_half = np.zeros((half, D), np.float32)
    in_maps = []
    for c in range(8):
        s, hf = c // 2, c % 2
        m = dict(shared)
        m["xm"] = np.ascontiguousarray(x_prompt[s, hf * half:(hf + 1) * half])
        m["xp"] = np.ascontiguousarray(x_prompt[s, 0:half]) if hf == 1 else zeros_half
        m["xs"] = np.ascontiguousarray(x_sample[2 * c:2 * c + 2].reshape(32, D))
        m["sc"] = np.ascontiguousarray(state_conv[0, 2 * c:2 * c + 2])
        m["sh"] = np.ascontiguousarray(state_hgrn[0, 2 * c:2 * c + 2])
        in_maps.append(m)
    if "nc" not in _NC_CACHE:
        _NC_CACHE["nc"] = build()
    res = run_bass_kernel_spmd(_NC_CACHE["nc"], in_maps, core_ids=list(range(8)))
    r = res.results
    y_prompt = np.zeros((B, SEQ, D), np.float32)
    y_sample = np.zeros((16, 16, D), np.float32)
    ncp = np.zeros((1, B, 30, CONV), np.float32)
    nhp = np.zeros((1, B, NH, 128, 128), np.float32)
    ncs = np.zeros((1, 16, 30, CONV), np.float32)
    nhs = np.zeros((1, 16, NH, 128, 128), np.float32)
    for c in range(8):
        s, hf = c // 2, c % 2
        y_prompt[s, hf * half:(hf + 1) * half] = r[c]["yp"]
        y_sample[2 * c:2 * c + 2] = r[c]["ys"].reshape(2, 16, D)
        ncs[0, 2 * c:2 * c + 2] = r[c]["ncs"]
        nhs[0, 2 * c:2 * c + 2] = r[c]["nhs"]
        if hf == 1:
            ncp[0, s] = r[c]["ncp"]
            nhp[0, s] = r[c]["nhp"]
    return (y_prompt, y_sample, ncp, nhp, ncs, nhs)
```

```python
import contextlib
import numpy as np
import concourse.bass as bass
import concourse.mybir as mybir
from concourse.bass_utils import run_bass_kernel_spmd

F32 = mybir.dt.float32
BF16 = mybir.dt.bfloat16
AF = mybir.ActivationFunctionType
ALU = mybir.AluOpType

D = 2048
KT = 16
CONV = 1024
NH = 8
DFF = 8192
IN_COLS = 6144
CW = 31
EPS = 1e-6
TW = 512
NPV = 72 + 248


class Op:
    __slots__ = ("eng", "fn", "deps", "sem", "val", "know", "flag", "is_dma", "waits")


class Sched:
    ENGS = ("pe", "act", "dve", "pool", "sp")
    EPOCH = 8000

    def __init__(self, nc):
        self.nc = nc
        self.all = []
        self.last_w = {}
        self.readers = {}
        self.dma_last = {}
        self.dma_count = {}

    def capture(self):
        self._cap = []

    def end_capture(self):
        lst = self._cap
        self._cap = None
        return lst

    def replay(self, *lists, span=None):
        items = []
        for li, lst in enumerate(lists):
            lo, hi = (span[li] if span else (0.0, 1.0))
            for i, a in enumerate(lst):
                items.append((lo + (hi - lo) * (i + 0.5) / len(lst), li, i, a))
        items.sort(key=lambda x: (x[0], x[1], x[2]))
        for _, _, _, a in items:
            self.op(*a)

    def op(self, eng, fn, reads=(), writes=(), dma=None, ndma=1):
        if getattr(self, "_cap", None) is not None:
            self._cap.append((eng, fn, tuple(reads), tuple(writes), dma, ndma))
            return None
        o = Op()
        o.eng = eng
        o.fn = fn
        o.deps = set()
        o.is_dma = dma is not None
        o.flag = False
        o.know = None
        for k in reads:
            w = self.last_w.get(k)
            if w is not None:
                o.deps.add(w)
            if isinstance(k, tuple) and k[0] == "big":
                for r in self.readers.get(k, ()):
                    if r.eng != eng:
                        o.deps.add(r)
        for k in writes:
            w = self.last_w.get(k)
            if w is not None:
                o.deps.add(w)
            for r in self.readers.get(k, ()):
                o.deps.add(r)
        for k in reads:
            self.readers.setdefault(k, []).append(o)
        for k in writes:
            self.last_w[k] = o
            self.readers[k] = []
        if o.is_dma:
            p = self.dma_last.get(dma)
            if p is not None:
                o.deps.add(p)
            self.dma_last[dma] = o
            c = self.dma_count.get(dma, 0) + ndma
            self.dma_count[dma] = c
            o.sem = ("dma", dma)
            o.val = 16 * c
            o.flag = True
        o.deps.discard(o)
        self.all.append(o)
        return o

    def emit(self):
        nc = self.nc
        for o in self.all:
            for d in o.deps:
                if o.eng == "pe" and d.eng == "pe" and not d.is_dma:
                    continue
                d.flag = True
        cnt = {e: 0 for e in self.ENGS}
        for o in self.all:
            if o.is_dma:
                continue
            if o.flag:
                c = cnt[o.eng]
                o.sem = ("eng", o.eng, c // self.EPOCH)
                o.val = (c % self.EPOCH) + 1
                cnt[o.eng] = c + 1
        know = {e: {} for e in self.ENGS}
        for o in self.all:
            kn = know[o.eng]
            wm = {}
            dd = [d for d in o.deps if not (o.eng == "pe" and d.eng == "pe" and not d.is_dma)]
            for d in sorted(dd, key=lambda d: (str(d.sem), d.val)):
                if kn.get(d.sem, 0) < d.val:
                    wm[d.sem] = max(wm.get(d.sem, 0), d.val)
                    kn[d.sem] = d.val
                    if d.know:
                        for s, v in d.know.items():
                            if kn.get(s, 0) < v:
                                kn[s] = v
            o.waits = list(wm.items())
            if o.flag:
                o.know = dict(kn)
        final = [(("dma", k), 16 * c) for k, c in self.dma_count.items()]
        keys = []
        seen = set()
        for o in self.all:
            if o.flag and o.sem not in seen:
                seen.add(o.sem)
                keys.append(o.sem)
        self.nsem = len(keys)
        with contextlib.ExitStack() as st:
            sems = {}
            for i, k in enumerate(keys):
                sems[k] = st.enter_context(nc.semaphore("s%d" % i))
            block = st.enter_context(nc.Block())
            engobj = {"pe": "tensor", "act": "scalar", "dve": "vector", "pool": "gpsimd", "sp": "sync"}
            for e in self.ENGS:
                ops = [o for o in self.all if o.eng == e]
                fin = final if e == "sp" else []
                if not ops and not fin:
                    continue

                def body(eng, ops=ops, fin=fin):
                    for o in ops:
                        for s, v in o.waits:
                            eng.wait_ge(sems[s], v)
                        ins = o.fn(eng)
                        if o.flag:
                            if o.is_dma:
                                for i_ in ins:
                                    i_.then_inc(sems[o.sem], 16)
                            else:
                                ins.then_inc(sems[o.sem], 1)
                    for s, v in fin:
                        eng.wait_ge(sems[s], v)

                getattr(block, engobj[e])(body)


def build(n_prev=4, n_main=4, sample=True, ring_slots=3):
    nc = bass.Bass("TRN2", target_bir_lowering=False)
    dt_in = lambda n, sh: nc.dram_tensor(n, sh, F32, kind="ExternalInput").ap()
    dt_out = lambda n, sh: nc.dram_tensor(n, sh, F32, kind="ExternalOutput").ap()
    xm = dt_in("xm", [n_main * TW, D])
    xp = dt_in("xp", [max(n_prev, 1) * TW, D])
    xs = dt_in("xs", [32, D])
    sc = dt_in("sc", [2, 30, CONV])
    sh = dt_in("sh", [2, NH, 128, 128])
    w_in = dt_in("w_in", [D, IN_COLS])
    w_in_hm = dt_in("w_in_hm", [D, IN_COLS])
    w_out = dt_in("w_out", [D, D])
    w_up = dt_in("w_up", [D, DFF])
    w_down = dt_in("w_down", [DFF, D])
    pvec_d = dt_in("pvec", [128, NPV])
    gfin_d = dt_in("gfin", [D])
    hg_d = dt_in("hg", [CONV])
    yp = dt_out("yp", [n_main * TW, D])
    ys = dt_out("ys", [32, D])
    ncp = dt_out("ncp", [30, CONV])
    nhp = dt_out("nhp", [NH, 128, 128])
    ncs = dt_out("ncs", [2, 30, CONV])
    nhs = dt_out("nhs", [2, NH, 128, 128])
    hs_scr = nc.dram_tensor("hs_scr", [32, D], F32, kind="Internal").ap()

    w_in_v = w_in.rearrange("(kt p) c -> p kt c", p=128)
    w_hm_v = w_in_hm.rearrange("(kt p) c -> p kt c", p=128)
    w_out_v = w_out.rearrange("(kt p) c -> p kt c", p=128)
    w_up_v = w_up.rearrange("(kt p) c -> p kt c", p=128)
    w_down_v = w_down.rearrange("(kt p) c -> p kt c", p=128)

    st = contextlib.ExitStack()
    with st:
        sb = lambda n, shp, dt: st.enter_context(nc.sbuf_tensor(n, shp, dt))
        ps = lambda n, shp, dt: st.enter_context(nc.psum_tensor(n, shp, dt))
        NS = ring_slots
        ring = [sb("ring%d" % i, [128, KT, 512], BF16) for i in range(NS)]
        hbuf = sb("hbuf", [128, 4, D], F32)
        xn = [sb("xn%d" % i, [128, D], BF16) for i in range(2)]
        actT = sb("actT", [128, KT, TW], BF16)
        tok1k = xn[1][0:32, :].bitcast(F32)
        mixT = sb("mixT", [128, KT, TW], BF16)
        gfin = sb("gfin_r", [128, D], F32)
        hgr = sb("hg_r", [128, CONV], F32)
        pvec = sb("pvec_s", [128, NPV], F32)
        ident = sb("ident", [128, 128], BF16)
        identf = sb("identf", [128, 128], F32)
        maskT = sb("maskT", [128, 128], F32)
        zeros = sb("zeros", [128, 128], F32)
        zeros4 = zeros[:, 0:1].to_broadcast([128, TW])
        ones = sb("ones", [128, 128], BF16)
        lbt = sb("lbt", [128, 4, NH], F32)
        dg = sb("dg", [128, CW, 128], BF16)
        ubuf = sb("ubuf", [128, 8, 30 + TW], BF16)
        ubs = sb("ubs", [128, 8, 2, 46], BF16)
        scr = sb("scr", [128, 8, 512], F32)
        cbf = [sb("cbf%d" % i, [128, TW], BF16) for i in range(2)]
        csq = [sb("csq%d" % i, [128, TW], BF16) for i in range(2)]
        cmu = sb("cmu", [128, TW], F32)
        crs = sb("crs", [128, TW], F32)
        tmpE2 = [sb("tmpE%d" % i, [128, TW], F32) for i in range(2)]
        utail = sb("utail", [128, 8, 32], F32)
        Sst = [sb("S%d" % i, [128, NH, 128], F32) for i in range(2)]
        Sbf = [sb("Sbf%d" % i, [128, NH, 128], BF16) for i in range(2)]
        vtok = [sb("vtok%d" % i, [128, 128], BF16) for i in range(2)]
        ktok = [sb("ktok%d" % i, [128, 128], BF16) for i in range(2)]
        attT = [sb("attT%d" % i, [128, 128], BF16) for i in range(2)]
        btok = [sb("btok%d" % i, [128, 128], BF16) for i in range(2)]
        gt = [sb("gt%d" % i, [128, 128], F32) for i in range(2)]
        jsm = sb("jsm", [128, 128], BF16)
        kq = [sb("kq%d" % i, [128, 3 * TW], BF16) for i in range(2)]
        actS = sb("actS", [128, KT, 32], BF16)
        hidS = sb("hidS", [128, KT, 32], BF16)
        pend = [sb("pend%d" % i, [128, 4], F32) for i in range(2)]
        sfx = [sb("sfx%d" % i, [128, 4], F32) for i in range(2)]
        stt = sb("stt", [128, 64], F32)

        big = [ps("bank%d" % i, [128, 512], F32) for i in range(8)]
        bbf = lambda b_: big[b_][:, :].bitcast(BF16)
        cur = {"big": [0, 1, 2], "i": 0}

        def nbig():
            lst = cur["big"]
            v = lst[cur["i"] % len(lst)]
            cur["i"] += 1
            return v
        TN = [3, 4]
        TH = [5, 6]
        IGB = [3, 4]
        HB = [2, 7]

        S = Sched(nc)
        cnt = {"ring": 0, "big": 0, "pT": 0, "alt": 0, "ig": 0, "xn": 0, "cb": 0, "stt": 0, "cb2": 0}

        def nxt(name, mod):
            v = cnt[name]
            cnt[name] = (v + 1) % mod if mod else v + 1
            return v

        gmixT = lambda k: k
        gmlpT = lambda k: 16 + k
        bdwT = lambda ct: pvec[:, 32 + ct:33 + ct]
        lngT = lambda ct: pvec[:, 40 + ct:41 + ct]
        lnbT = lambda ct: pvec[:, 48 + ct:49 + ct]
        wdwT = lambda ct, j: pvec[:, 72 + ct * CW + j:73 + ct * CW + j]
        lbv = lambda h: lbt[:, 1, h:h + 1]
        omlv = lambda h: lbt[:, 2, h:h + 1]
        nomlv = lambda h: lbt[:, 3, h:h + 1]

        S.op("sp", lambda q: [q.dma_start(out=pvec[:], in_=pvec_d)], writes=["pvec"], dma="pvec")
        S.op("sp", lambda q: [q.dma_start(out=gfin[:], in_=gfin_d.partition_broadcast(128))], writes=["gfin"], dma="gfin")
        S.op("sp", lambda q: [q.dma_start(out=hgr[:], in_=hg_d.partition_broadcast(128))], writes=["hgr"], dma="hgr")
        S.op("pool", lambda g: g.memset(identf[:], 1.0), writes=["identf"])
        S.op("pool", lambda g: g.affine_select(out=identf[:], in_=identf[:], pattern=[[1, 128]], compare_op=ALU.is_equal,
                                               fill=0.0, base=0, channel_multiplier=-1), reads=["identf"], writes=["identf"])
        S.op("pool", lambda g: g.memset(maskT[:], 1.0), writes=["maskT"])
        S.op("pool", lambda g: g.affine_select(out=maskT[:], in_=maskT[:], pattern=[[1, 128]], compare_op=ALU.is_ge,
                                               fill=0.0, base=0, channel_multiplier=-1), reads=["maskT"], writes=["maskT"])
        S.op("dve", lambda v: v.tensor_copy(out=ident[:], in_=identf[:]), reads=["identf"], writes=["ident"])
        S.op("dve", lambda v: v.memset(zeros[:], 0.0), writes=["zeros"])

        S.op("dve", lambda v: v.memset(ones[:], 1.0), writes=["ones"])
        S.op("dve", lambda v: v.memset(ubuf[:, :, 0:30], 0.0), writes=[("u", ct) for ct in range(8)])
        S.op("dve", lambda v: v.tensor_sub(out=lbt[:, 0, :], in0=pvec[:, 64:72], in1=pvec[:, 56:64]), reads=["pvec"], writes=["lb0"])
        S.op("act", lambda a: a.activation(out=lbt[:, 0, :], in_=lbt[:, 0, :], func=AF.Exp), reads=["lb0"], writes=["lb0"])
        S.op("dve", lambda v: v.tensor_scalar_add(out=lbt[:, 1, :], in0=lbt[:, 0, :], scalar1=1.0), reads=["lb0"], writes=["lb1"])
        S.op("dve", lambda v: v.reciprocal(out=lbt[:, 1, :], in_=lbt[:, 1, :]), reads=["lb1"], writes=["lb1"])
        S.op("dve", lambda v: v.tensor_mul(out=lbt[:, 2, :], in0=lbt[:, 0, :], in1=lbt[:, 1, :]), reads=["lb0", "lb1"], writes=["lb2"])
        S.op("dve", lambda v: v.tensor_scalar_mul(out=lbt[:, 3, :], in0=lbt[:, 2, :], scalar1=-1.0), reads=["lb2"], writes=["lb3"])
        LBK = ["lb1", "lb2", "lb3"]
        nln = sb("nln", [128, 16], F32)
        S.op("dve", lambda v: v.tensor_scalar_mul(out=nln[:], in0=pvec[:, 40:56], scalar1=-1.0), reads=["pvec"], writes=["nln"])
        nlngT = lambda ct: nln[:, ct:ct + 1]
        nlnbT = lambda ct: nln[:, 8 + ct:9 + ct]
        PH0 = [("pH", 0), ("pHo", 0), ("pHs", 0)]
        PH1 = [("pH", 1), ("pHo", 1), ("pHs", 1)]

        def sk(g):
            return [("scr", g)]

        def ring_load(pieces):
            s = nxt("ring", NS)

            def fn(g, s=s, pieces=pieces):
                out = []
                for (view, kt0, c0, n, sc0) in pieces:
                    out.append(g.dma_start(out=ring[s][:, :, sc0:sc0 + n], in_=view[:, kt0:kt0 + KT, c0:c0 + n]))
                return out
            S.op("pool", fn, writes=[("ring", s)], dma=("ring", s), ndma=len(pieces))
            return s

        def gemm_fm(s, c0, ncols, acols, psap, pskey, akeys, src_=None):
            a0, a1 = acols
            src_ = actT if src_ is None else src_

            def fn(t):
                last = None
                for k in range(KT):
                    last = t.matmul(psap, lhsT=ring[s][:, k, c0:c0 + ncols], rhs=src_[:, k, a0:a1], start=(k == 0), stop=(k == KT - 1))
                return last
            S.op("pe", fn, reads=[("ring", s)] + akeys, writes=[pskey])

        def gemm_tm(src, srckeys, s, c0, ncols, off, n, psap, pskey):
            def fn(t):
                last = None
                for k in range(KT):
                    last = t.matmul(psap, lhsT=src[:, k, off:off + n], rhs=ring[s][:, k, c0:c0 + ncols], start=(k == 0), stop=(k == KT - 1))
                return last
            S.op("pe", fn, reads=[("ring", s)] + srckeys, writes=[pskey])

        def rsqrt_stat(col, n, scale):
            S.op("act", lambda a: a.activation(out=stt[:n, col + 1:col + 2], in_=stt[:n, col:col + 1], func=AF.Ln, scale=scale, bias=EPS),
                 reads=[("stt", col)], writes=[("stt", col + 1)])
            S.op("act", lambda a: a.activation(out=stt[:n, col + 1:col + 2], in_=stt[:n, col + 1:col + 2], func=AF.Exp, scale=-0.5),
                 reads=[("stt", col + 1)], writes=[("stt", col + 1)])

        def sigmoid_from(psap, pskey, outap, outkey, shape_n):
            okl = outkey if isinstance(outkey, list) else [outkey]
            S.op("act", lambda a: a.activation(out=outap, in_=psap, func=AF.Exp, scale=-1.0), reads=[pskey], writes=okl)
            S.op("act", lambda a: a.activation(out=outap, in_=outap, func=AF.Ln, bias=1.0), reads=okl, writes=okl)
            S.op("act", lambda a: a.activation(out=outap, in_=outap, func=AF.Exp, scale=-1.0), reads=okl, writes=okl)

        def norm_to_actT(mtiles, gfun, only=None):
            for mi, (off, n) in enumerate(mtiles):
                if only is not None and mi != only:
                    continue
                hk = [("h", mi, cb) for cb in range(4)]
                xi = nxt("xn", 2)
                sc_ = 2 * nxt("stt", 32)
                S.op("act", lambda a, mi=mi, n=n, xi=xi, sc_=sc_: a.activation(out=xn[xi][:n, :], in_=hbuf[:n, mi, :], func=AF.Square,
                                                                         accum_out=stt[:n, sc_:sc_ + 1]),
                     reads=hk, writes=[("xn", xi), ("stt", sc_)])
                rsqrt_stat(sc_, n, 1.0 / D)
                S.op("dve", lambda v, mi=mi, n=n, xi=xi, sc_=sc_: v.tensor_scalar_mul(out=xn[xi][:n, :], in0=hbuf[:n, mi, :],
                                                                               scalar1=stt[:n, sc_ + 1:sc_ + 2]),
                     reads=hk + [("stt", sc_ + 1)], writes=[("xn", xi)])
                for kg in range(4):
                    hf = TN[nxt("pT", 2)]

                    def tfn(t, n=n, xi=xi, kg=kg, hf=hf):
                        last = None
                        for j in range(4):
                            k = kg * 4 + j
                            last = t.transpose(out=bbf(hf)[:, j * 128:j * 128 + n], in_=xn[xi][:n, k * 128:(k + 1) * 128], identity=ident[:n, :n])
                        return last
                    S.op("pe", tfn, reads=[("xn", xi), "ident"], writes=[("big", hf)])
                    gsl = gfun(kg * 4)
                    g4 = pvec[:, gsl:gsl + 4].unsqueeze(2).to_broadcast([128, 4, n])
                    S.op("dve", lambda v, n=n, off=off, kg=kg, hf=hf, g4=g4: v.tensor_tensor(
                        out=actT[:, kg * 4:kg * 4 + 4, off:off + n],
                        in0=bbf(hf)[:, 0:512].rearrange("p (j c) -> p j c", c=128)[:, :, 0:n], in1=g4, op=ALU.mult),
                        reads=[("big", hf), "pvec"], writes=[("actT", mi, kg * 4 + j) for j in range(4)])

        def process_tb(mode, xsrc, T, mtiles, sidx, out_y=None, last_main=False, part_a=False, extra=False):
            nm = len(mtiles)
            akeys = [("actT", mi, k) for mi in range(nm) for k in range(KT)]
            full = mode in ("main", "sample")
            for mi, (off, n) in enumerate(mtiles):
                S.op("sp", lambda q, mi=mi, off=off, n=n: [q.dma_start(out=hbuf[:n, mi, :], in_=xsrc[off:off + n, :])],
                     writes=[("h", mi, cb) for cb in range(4)], dma=("h", mi))
            norm_to_actT(mtiles, gmixT)

            cur["big"] = [0, 1, 2]
            if full or mode == "prev_last":
                cur["big"] = [0, 1, 2, 3, 4, 5] if full else [0, 1, 2]
                if mode == "prev_last":
                    ccols = (T - 32, T)
                else:
                    ccols = (0, T)
                cn = ccols[1] - ccols[0]
                slots = {}

                def glu(ct):
                    p, jj = divmod(ct, 2)
                    if jj == 0:
                        slots[p] = ring_load([(w_hm_v, 0, p * 512, 512, 0)])
                    s = slots[p]
                    ba = nbig()
                    gemm_fm(s, jj * 128, 128, ccols, big[ba][:, :cn], ("big", ba), akeys)
                    bg = nbig()
                    gemm_fm(s, 256 + jj * 128, 128, ccols, big[bg][:, :cn], ("big", bg), akeys)
                    tE = scr[:, ct, :]
                    tk = sk(ct)
                    sigmoid_from(big[bg][:, :cn], ("big", bg), tE[:, :cn], tk, cn)
                    if mode == "prev_last":
                        S.op("dve", lambda v: v.tensor_mul(out=ubuf[:, ct, 0:30], in0=big[ba][:, 2:32], in1=tE[:, 2:32]),
                             reads=[("big", ba)] + tk, writes=[("u", ct)])
                    elif mode == "main":
                        S.op("dve", lambda v: v.tensor_mul(out=ubuf[:, ct, 30:30 + T], in0=big[ba][:, :T], in1=tE[:, :T]),
                             reads=[("big", ba)] + tk, writes=[("u", ct)])
                        if last_main:
                            S.op("dve", lambda v: v.tensor_mul(out=utail[:, ct, :], in0=big[ba][:, T - 32:T], in1=tE[:, T - 32:T]),
                                 reads=[("big", ba)] + tk, writes=[("utail", ct)])
                    else:
                        for si in range(2):
                            S.op("dve", lambda v, si=si: v.tensor_mul(out=ubs[:, ct, si, 30:46], in0=big[ba][:, si * 16:si * 16 + 16],
                                                                     in1=tE[:, si * 16:si * 16 + 16]),
                                 reads=[("big", ba)] + tk, writes=[("us", ct, si)])
                        S.op("dve", lambda v: v.tensor_mul(out=utail[:, ct, :], in0=big[ba][:, 0:32], in1=tE[:, 0:32]),
                             reads=[("big", ba)] + tk, writes=[("utail", ct)])

                def conv_diag(ct):
                    S.op("dve", lambda v: v.tensor_tensor(out=dg[:], in0=identf[:].unsqueeze(1).to_broadcast([128, CW, 128]),
                                                          in1=pvec[:, 72 + ct * CW:72 + (ct + 1) * CW].unsqueeze(2).to_broadcast([128, CW, 128]),
                                                          op=ALU.mult),
                         reads=["identf", "pvec"], writes=[("dg", j) for j in range(CW)])

                def conv_rest(ct):
                    bc = nbig()

                    def cfn(t):
                        last = None
                        if mode == "main":
                            for j in range(CW):
                                last = t.matmul(big[bc][:, :T], lhsT=dg[:, j, :], rhs=ubuf[:, ct, j:j + T], start=(j == 0), stop=(j == CW - 1))
                        else:
                            for si in range(2):
                                for j in range(CW):
                                    last = t.matmul(big[bc][:, si * 16:si * 16 + 16], lhsT=dg[:, j, :], rhs=ubs[:, ct, si, j:j + 16],
                                                    start=(j == 0), stop=(j == CW - 1))
                        return last
                    ukeys = [("u", ct)] if mode == "main" else [("us", ct, 0), ("us", ct, 1)]
                    S.op("pe", cfn, reads=ukeys + [("dg", j) for j in range(CW)], writes=[("big", bc)])
                    if mode == "main":
                        S.op("dve", lambda v: v.tensor_copy(out=ubuf[:, ct, 0:30], in_=ubuf[:, ct, T:T + 30]),
                             reads=[("u", ct)], writes=[("u", ct)])
                    S.op("act", lambda a_: a_.activation(out=scr[:, ct, :T], in_=big[bc][:, :T], func=AF.Identity, bias=bdwT(ct)),
                         reads=[("big", bc), "pvec"], writes=sk(ct))
                    ci = nxt("cb", 2)
                    S.op("act", lambda a_: a_.activation(out=csq[ci][:, :T], in_=big[bc][:, :T], func=AF.Square, bias=bdwT(ct)),
                         reads=[("big", bc), "pvec"], writes=[("csq", ci)])
                    S.op("dve", lambda v: v.tensor_copy(out=cbf[ci][:, :T], in_=scr[:, ct, :T]),
                         reads=sk(ct), writes=[("cbf", ci)])
                    return ci

                def conv_stats(ct, ci):
                    S.op("pe", lambda t: t.matmul(big[6][:, :T], lhsT=ones[:], rhs=cbf[ci][:, :T], start=(ct == 0), stop=(ct == 7)),
                         reads=[("cbf", ci), "ones"], writes=[("big", 6)])
                    S.op("pe", lambda t: t.matmul(big[7][:, :T], lhsT=ones[:], rhs=csq[ci][:, :T], start=(ct == 0), stop=(ct == 7)),
                         reads=[("csq", ci), "ones"], writes=[("big", 7)])

                glu(0)
                pend_stats = None
                for ct in range(8):
                    if mode != "prev_last":
                        conv_diag(ct)
                    if ct + 1 < 8:
                        glu(ct + 1)
                    if mode != "prev_last":
                        ci_ = conv_rest(ct)
                        if pend_stats is not None:
                            conv_stats(*pend_stats)
                        pend_stats = (ct, ci_)
                if pend_stats is not None:
                    conv_stats(*pend_stats)
                if full:
                    pXsq = big[7]
                    S.op("act", lambda a: a.activation(out=cmu[:, :T], in_=big[6][:, :T], func=AF.Copy, scale=1.0 / CONV),
                         reads=[("big", 6)], writes=["cmu"])
                    S.op("dve", lambda v: v.tensor_mul(out=crs[:, :T], in0=cmu[:, :T], in1=cmu[:, :T]), reads=["cmu"], writes=["crs"])
                    S.op("dve", lambda v: v.scalar_tensor_tensor(out=crs[:, :T], in0=pXsq[:, :T], scalar=1.0 / CONV, in1=crs[:, :T],
                                                                 op0=ALU.mult, op1=ALU.subtract),
                         reads=[("big", 7), "crs"], writes=["crs"])
                    S.op("act", lambda a: a.activation(out=crs[:, :T], in_=crs[:, :T], func=AF.Ln, bias=EPS), reads=["crs"], writes=["crs"])
                    S.op("act", lambda a: a.activation(out=crs[:, :T], in_=crs[:, :T], func=AF.Exp, scale=-0.5), reads=["crs"], writes=["crs"])
                    def ln_ct(ct):
                        tE_ = tmpE2[ct % 2]
                        tEk = ("tmpE", ct % 2)
                        S.op("dve", lambda v, ct=ct: v.tensor_sub(out=scr[:, ct, :T], in0=scr[:, ct, :T], in1=cmu[:, :T]),
                             reads=sk(ct) + ["cmu"], writes=sk(ct))
                        S.op("dve", lambda v, ct=ct: v.tensor_mul(out=scr[:, ct, :T], in0=scr[:, ct, :T], in1=crs[:, :T]),
                             reads=sk(ct) + ["crs"], writes=sk(ct))
                        S.op("act", lambda a, ct=ct, tE_=tE_: a.activation(out=tE_[:, :T], in_=scr[:, ct, :T], func=AF.Exp, scale=nlngT(ct), bias=nlnbT(ct)),
                             reads=sk(ct) + ["nln"], writes=[tEk])
                        S.op("act", lambda a, ct=ct: a.activation(out=scr[:, ct, :T], in_=scr[:, ct, :T], func=AF.Identity, scale=lngT(ct), bias=lnbT(ct)),
                             reads=sk(ct) + ["pvec"], writes=sk(ct))
                        S.op("act", lambda a, tE_=tE_: a.activation(out=tE_[:, :T], in_=tE_[:, :T], func=AF.Ln, bias=1.0), reads=[tEk], writes=[tEk])
                        S.op("act", lambda a, tE_=tE_: a.activation(out=tE_[:, :T], in_=tE_[:, :T], func=AF.Exp, scale=-1.0), reads=[tEk], writes=[tEk])
                        S.op("dve", lambda v, ct=ct, tE_=tE_: v.tensor_mul(out=mixT[:, ct, :T], in0=scr[:, ct, :T], in1=tE_[:, :T]),
                             reads=sk(ct) + [tEk], writes=[("mixT", ct)])

                    for ct in (4, 5, 6, 7):
                        ln_ct(ct)

            if not full:
                cur["big"] = [0, 1]
                X = lambda g: scr[:, g, :T]
                vall = mixT[:, :, :].rearrange("p a b -> p (a b)")
                kt4 = [scr[:, 6, :].bitcast(BF16), scr[:, 7, :].bitcast(BF16)]
                for half in range(2):
                    s_i = ring_load([(w_in_v, 0, 4096 + half * 512, 512, 0)])
                    for mi, (off, n) in enumerate(mtiles):
                        b = IGB[nxt("ig", 2)]
                        gemm_tm(actT, [("actT", mi, k) for k in range(KT)], s_i, 0, 512, off, n, big[b][:n, :], ("big", b))
                        S.op("act", lambda a, b=b, n=n, mi=mi, half=half: a.activation(
                            out=vall[:n, mi * 1024 + half * 512:mi * 1024 + half * 512 + 512], in_=big[b][:n, :], func=AF.Copy),
                            reads=[("big", b)], writes=[("mixT", mi * 2 + half)])
                fs = {}

                def fgemm(h):
                    half, j = divmod(h, 4)
                    if j == 0:
                        fs[half] = ring_load([(w_in_v, 0, 3072 + half * 512, 512, 0)])
                    bf_ = nbig()
                    gemm_fm(fs[half], j * 128, 128, (0, T), big[bf_][:, :T], ("big", bf_), akeys)
                    return bf_

                def pgates(h, bf_):
                    hp = h % 2
                    KE = kq[hp][:, TW:TW + T]
                    kek = ("KE", hp)
                    ga, gb, gc = 3 * hp, 3 * hp + 1, 3 * hp + 2
                    A, B, C = scr[:, ga, :T], scr[:, gb, :T], scr[:, gc, :T]
                    ka, kb, kc = ("scr", ga), ("scr", gb), ("scr", gc)
                    sigmoid_from(big[bf_][:, :T], ("big", bf_), A, ka, T)
                    S.op("act", lambda a: a.activation(out=B, in_=A, func=AF.Ln, scale=omlv(h), bias=lbv(h)),
                         reads=[ka] + LBK, writes=[kb])
                    S.op("dve", lambda v: v.tensor_scalar(out=C, in0=A, scalar1=nomlv(h), scalar2=omlv(h), op0=ALU.mult, op1=ALU.add),
                         reads=[ka] + LBK, writes=[kc])
                    S.op("dve", lambda v: v.tensor_tensor_scan(out=A, data0=B, data1=zeros[:, 0:1].to_broadcast([128, T]), initial=0.0,
                                                               op0=ALU.add, op1=ALU.add),
                         reads=[kb, "zeros"], writes=[ka])
                    S.op("act", lambda a: a.activation(out=B, in_=A, func=AF.Exp, scale=-1.0, bias=scr[:, ga, T - 1:T]),
                         reads=[ka], writes=[kb])
                    S.op("act", lambda a: a.activation(out=sfx[hp][:, 0:1], in_=scr[:, ga, T - 1:T], func=AF.Exp),
                         reads=[ka], writes=[("sfx", hp)])
                    S.op("dve", lambda v: v.tensor_mul(out=KE, in0=C, in1=B), reads=[kc, kb], writes=[kek])

                def pupdate(h):
                    hp = h % 2
                    KE = kq[hp][:, TW:TW + T]
                    kek = ("KE", hp)
                    al = nxt("alt", 2)
                    hb = HB[al]
                    hf = TH[nxt("pT", 2)]

                    def tfn(t):
                        last = None
                        for c, (off, n) in enumerate(mtiles):
                            last = t.transpose(out=bbf(hf)[:n, c * 128:(c + 1) * 128], in_=KE[:, off:off + n], identity=ident[:])
                        return last
                    S.op("pe", tfn, reads=[kek, "ident"], writes=[("big", hf)])
                    S.op("act", lambda a: a.activation(out=kt4[al][:, 0:512], in_=bbf(hf)[:, 0:512], func=AF.Copy),
                         reads=[("big", hf)], writes=sk(6 + al))

                    def sfn(t):
                        last = None
                        for c, (off, n) in enumerate(mtiles):
                            last = t.matmul(big[hb][:, 256:384], lhsT=kt4[al][:n, c * 128:(c + 1) * 128],
                                            rhs=vall[:n, c * 1024 + h * 128:c * 1024 + (h + 1) * 128], start=(c == 0), stop=(c == nm - 1))
                        return last
                    S.op("pe", sfn, reads=sk(6 + al) + [("mixT", c * 2 + h // 4) for c in range(nm)], writes=[("big", hb)])
                    S.op("dve", lambda v: v.scalar_tensor_tensor(
                        out=Sst[0][:, h, :], in0=Sst[0][:, h, :], scalar=sfx[hp][:, 0:1], in1=big[hb][:, 256:384],
                        op0=ALU.mult, op1=ALU.add),
                        reads=[("S", 0, h), ("sfx", hp), ("big", hb)], writes=[("S", 0, h)])
                    S.op("act", lambda a: a.activation(out=Sbf[0][:, h, :], in_=Sst[0][:, h, :], func=AF.Copy),
                         reads=[("S", 0, h)], writes=[("Sbf", 0, h)])

                fb = {0: fgemm(0)}
                pgates(0, fb[0])
                fb[1] = fgemm(1)
                for h in range(NH):
                    S.capture()
                    pupdate(h)
                    la = S.end_capture()
                    S.capture()
                    if h + 2 < NH:
                        fb[h + 2] = fgemm(h + 2)
                    if h + 1 < NH:
                        pgates(h + 1, fb[h + 1])
                    lb_ = S.end_capture()
                    if lb_:
                        S.replay(la, lb_)
                    else:
                        S.replay(la)
                return

            cur["big"] = [0, 1]
            X = lambda g: scr[:, g, :T]
            ncol = 256 if full else 128
            csz = mtiles[0][1]

            def head_gemm(h):
                s = ring_load([(w_hm_v, 0, 2048 + h * 512, 512, 0)])
                bf_ = nbig()
                gemm_fm(s, 128, 128, (0, T), big[bf_][:, :T], ("big", bf_), akeys)
                bq = None
                if full:
                    bq = nbig()
                    gemm_fm(s, 0, 128, (0, T), big[bq][:, :T], ("big", bq), akeys)
                return s, bf_, bq

            def head_gates(h, bf_, bq):
                hp = h % 2
                KD, KE, QD = kq[hp][:, 0:T], kq[hp][:, TW:TW + T], kq[hp][:, 2 * TW:2 * TW + T]
                kdk, kek, qdk = ("KD", hp), ("KE", hp), ("QD", hp)
                G = lambda i: scr[:, 4 + i, :T]
                gk = lambda i: ("scr", 4 + i)
                sigmoid_from(big[bf_][:, :T], ("big", bf_), G(0), gk(0), T)
                S.op("act", lambda a: a.activation(out=G(1), in_=G(0), func=AF.Ln, scale=omlv(h), bias=lbv(h)),
                     reads=[gk(0)] + LBK, writes=[gk(1)])
                S.op("dve", lambda v: v.tensor_scalar(out=G(2), in0=G(0), scalar1=nomlv(h), scalar2=omlv(h), op0=ALU.mult, op1=ALU.add),
                     reads=[gk(0)] + LBK, writes=[gk(2)])
                for mi, (off, n) in enumerate(mtiles):
                    S.op("dve", lambda v, off=off, n=n: v.tensor_tensor_scan(out=scr[:, 7, off:off + n], data0=scr[:, 5, off:off + n],
                                                                             data1=zeros[:, 0:n], initial=0.0, op0=ALU.add, op1=ALU.add),
                         reads=[gk(1), "zeros"], writes=[gk(3)])
                S.op("act", lambda a: a.activation(out=G(0), in_=G(3), func=AF.Exp, scale=-1.0), reads=[gk(3)], writes=[gk(0)])
                S.op("act", lambda a: a.activation(out=G(3), in_=G(3), func=AF.Exp), reads=[gk(3)], writes=[gk(3)])
                S.op("dve", lambda v: v.tensor_copy(out=pend[hp][:, 0:nm], in_=scr[:, 7, csz - 1:T:csz]), reads=[gk(3)], writes=[("pend", hp)])
                S.op("dve", lambda v: v.tensor_mul(out=KD, in0=G(2), in1=G(0)), reads=[gk(2), gk(0)], writes=[kdk])
                S.op("dve", lambda v: v.tensor_tensor(out=KE.rearrange("p (c n) -> p c n", n=csz), in0=KD.rearrange("p (c n) -> p c n", n=csz),
                                                      in1=pend[hp][:, 0:nm].unsqueeze(2).to_broadcast([128, nm, csz]), op=ALU.mult),
                     reads=[kdk, ("pend", hp)], writes=[kek])
                if full:
                    sigmoid_from(big[bq][:, :T], ("big", bq), G(1), gk(1), T)
                    S.op("dve", lambda v: v.tensor_mul(out=G(1), in0=big[bq][:, :T], in1=G(1)), reads=[("big", bq), gk(1)], writes=[gk(1)])
                    S.op("dve", lambda v: v.tensor_mul(out=QD, in0=G(1), in1=G(3)), reads=[gk(1), gk(3)], writes=[qdk])

            def chunk_s1(h, s, mi):
                off, n = mtiles[mi]
                hp = h % 2
                KD, KE, QD = kq[hp][:, 0:T], kq[hp][:, TW:TW + T], kq[hp][:, 2 * TW:2 * TW + T]
                kdk, kek, qdk = ("KD", hp), ("KE", hp), ("QD", hp)
                al = nxt("alt", 2)
                hb = HB[al]
                ig = IGB[nxt("ig", 2)]
                gemm_tm(actT, [("actT", mi, k) for k in range(KT)], s, 256, ncol, off, n, big[ig][:n, 0:ncol], ("big", ig))
                S.op("dve", lambda v: v.tensor_copy(out=vtok[al][:n, :], in_=big[ig][:n, 0:128]),
                     reads=[("big", ig)], writes=[("vtok", al)])
                if full:
                    sigmoid_from(big[ig][:n, 128:256], ("big", ig), gt[al][:n, :], ("gt", al), n)
                    S.op("dve", lambda v: v.tensor_mul(out=gt[al][:n, :], in0=big[ig][:n, 128:256], in1=gt[al][:n, :]),
                         reads=[("big", ig), ("gt", al)], writes=[("gt", al)])
                    S.op("dve", lambda v: v.tensor_mul(out=gt[al][:n, :], in0=gt[al][:n, :], in1=hgr[:n, h * 128:(h + 1) * 128]),
                         reads=[("gt", al), "hgr"], writes=[("gt", al)])
                hf = TH[nxt("pT", 2)]
                S.op("pe", lambda t: t.transpose(out=bbf(hf)[:n, 0:128], in_=KE[:, off:off + n], identity=ident[:]),
                     reads=[kek, "ident"], writes=[("big", hf)])
                S.op("act", lambda a: a.activation(out=ktok[al][:n, :], in_=bbf(hf)[:n, 0:128], func=AF.Copy),
                     reads=[("big", hf)], writes=[("ktok", al)])
                if full:
                    S.op("pe", lambda t: t.matmul(big[hb][:n, 0:n], lhsT=KD[:, off:off + n], rhs=QD[:, off:off + n], start=True, stop=True),
                         reads=[kdk, qdk], writes=[("big", hb)])
                    S.op("dve", lambda v: v.tensor_mul(out=attT[al][:n, :n], in0=big[hb][:n, 0:n], in1=maskT[:n, :n]),
                         reads=[("big", hb), "maskT"], writes=[("attT", al)])
                return al, hb

            def chunk_s2(h, mi, al, hb):
                off, n = mtiles[mi]
                si = sidx[mi]
                hp = h % 2
                QD = kq[hp][:, 2 * TW:2 * TW + T]
                qdk = ("QD", hp)
                if full:
                    def ofn(t):
                        t.matmul(big[hb][:n, 128:256], lhsT=attT[al][:n, :n], rhs=vtok[al][:n, :], start=True, stop=False)
                        return t.matmul(big[hb][:n, 128:256], lhsT=QD[:, off:off + n], rhs=Sbf[si][:, h, :], start=False, stop=True)
                    S.op("pe", ofn, reads=[("attT", al), ("vtok", al), qdk, ("Sbf", si, h)], writes=[("big", hb)])
                S.op("pe", lambda t: t.matmul(big[hb][:, 256:384], lhsT=ktok[al][:n, :], rhs=vtok[al][:n, :], start=True, stop=True),
                     reads=[("ktok", al), ("vtok", al)], writes=[("big", hb)])
                S.op("dve", lambda v: v.scalar_tensor_tensor(
                    out=Sst[si][:, h, :], in0=Sst[si][:, h, :], scalar=pend[hp][:, mi:mi + 1], in1=big[hb][:, 256:384],
                    op0=ALU.mult, op1=ALU.add),
                    reads=[("S", si, h), ("pend", hp), ("big", hb)], writes=[("S", si, h)])
                def sbf_copy():
                    S.op("act", lambda a: a.activation(out=Sbf[si][:, h, :], in_=Sst[si][:, h, :], func=AF.Copy),
                         reads=[("S", si, h)], writes=[("Sbf", si, h)])
                if not full:
                    sbf_copy()
                if full:
                    sc_ = 2 * nxt("stt", 32)
                    S.op("act", lambda a: a.activation(out=jsm[:n, :], in_=big[hb][:n, 128:256], func=AF.Square, accum_out=stt[:n, sc_:sc_ + 1]),
                         reads=[("big", hb)], writes=["jsm", ("stt", sc_)])
                    rsqrt_stat(sc_, n, 1.0 / 128)
                    S.op("dve", lambda v: v.scalar_tensor_tensor(
                        out=btok[al][:n, :], in0=big[hb][:n, 128:256], scalar=stt[:n, sc_ + 1:sc_ + 2], in1=gt[al][:n, :],
                        op0=ALU.mult, op1=ALU.mult),
                        reads=[("big", hb), ("stt", sc_ + 1), ("gt", al)], writes=[("btok", al)])
                    sbf_copy()

            def chunk_s2b(h, mi, al, hb):
                off, n = mtiles[mi]
                if full:
                    hf2 = TH[nxt("pT", 2)]
                    S.op("pe", lambda t: t.transpose(out=bbf(hf2)[:, 0:n], in_=btok[al][:n, :], identity=ident[:n, :n]),
                         reads=[("btok", al), "ident"], writes=[("big", hf2)])
                    S.op("dve", lambda v: v.tensor_copy(out=mixT[:, 8 + h, off:off + n], in_=bbf(hf2)[:, 0:n]),
                         reads=[("big", hf2)], writes=[("mixT", 8 + h)])

            def chunks(h, s):
                ctxs = {0: chunk_s1(h, s, 0)}
                for mi in range(nm):
                    if mi + 1 < nm:
                        ctxs[mi + 1] = chunk_s1(h, s, mi + 1)
                    if mi >= 1:
                        chunk_s2b(h, mi - 1, *ctxs[mi - 1])
                    chunk_s2(h, mi, *ctxs[mi])
                chunk_s2b(h, nm - 1, *ctxs[nm - 1])

            g = {0: head_gemm(0)}
            head_gates(0, g[0][1], g[0][2])
            if NH > 1:
                g[1] = head_gemm(1)
            for h in range(NH):
                S.capture()
                chunks(h, g[h][0])
                la = S.end_capture()
                S.capture()
                if h + 1 < NH:
                    head_gates(h + 1, g[h + 1][1], g[h + 1][2])
                if h + 2 < NH:
                    g[h + 2] = head_gemm(h + 2)
                lb_ = S.end_capture()
                lists = [la] + ([lb_] if lb_ else [])
                if h == 0:
                    S.capture()
                    for ct in (0, 1, 2, 3):
                        ln_ct(ct)
                    lists.append(S.end_capture())
                S.replay(*lists, span=[(0.0, 1.0), (0.0, 0.7)] + [(0.0, 1.0)] * (len(lists) - 2) if len(lists) >= 2 else None)
            if not full:
                return

            cur["big"] = [0, 1, 2, 3, 4, 5]
            mkeys = [("mixT", k) for k in range(KT)]
            for cb in range(4):
                s = ring_load([(w_out_v, 0, cb * 512, 512, 0)])
                for mi, (off, n) in enumerate(mtiles):
                    b = nbig()
                    gemm_tm(mixT, mkeys, s, 0, 512, off, n, big[b][:n, :], ("big", b))
                    S.op("dve", lambda v, mi=mi, n=n, cb=cb, b=b: v.tensor_add(out=hbuf[:n, mi, cb * 512:(cb + 1) * 512],
                                                                              in0=hbuf[:n, mi, cb * 512:(cb + 1) * 512], in1=big[b][:n, :]),
                         reads=[("h", mi, cb), ("big", b)], writes=[("h", mi, cb)])
                    if cb == 3:
                        norm_to_actT(mtiles, gmlpT, only=mi)
            if part_a:
                S.op("dve", lambda v: v.tensor_copy(out=actS[:, :, 0:T], in_=actT[:, :, 0:T]), reads=akeys, writes=["actS"])
                for mi, (off, n) in enumerate(mtiles):
                    S.op("sp", lambda q, mi=mi, off=off, n=n: [q.dma_start(out=hs_scr[off:off + n, :], in_=hbuf[:n, mi, :])],
                         reads=[("h", mi, cb) for cb in range(4)], writes=["hs_scr"], dma=("hss", mi))
                return
            def final_norm_store(mi, n, dst):
                hk = [("h", mi, cb) for cb in range(4)]
                xi = nxt("xn", 2)
                sc_ = 2 * nxt("stt", 32)
                S.op("act", lambda a: a.activation(out=xn[xi][:n, :], in_=hbuf[:n, mi, :], func=AF.Square, accum_out=stt[:n, sc_:sc_ + 1]),
                     reads=hk, writes=[("xn", xi), ("stt", sc_)])
                rsqrt_stat(sc_, n, 1.0 / D)
                S.op("dve", lambda v: v.scalar_tensor_tensor(out=hbuf[:n, mi, :], in0=hbuf[:n, mi, :], scalar=stt[:n, sc_ + 1:sc_ + 2],
                                                             in1=gfin[:n, :], op0=ALU.mult, op1=ALU.mult),
                     reads=hk + [("stt", sc_ + 1), "gfin"], writes=hk)
                S.op("sp", lambda q: [q.dma_start(out=dst, in_=hbuf[:n, mi, :])], reads=hk, dma=("yo", mi))

            if extra:
                cur["big"] = [0, 1, 2, 3]
            hkeys = [("hidS", k) for k in range(KT)]
            for qf in range(4):
                for sbk in range(4):
                    s = ring_load([(w_up_v, 0, qf * 2048 + sbk * 512, 512, 0)])
                    for j in range(4):
                        b = nbig()
                        gemm_fm(s, j * 128, 128, (0, T), big[b][:, :T], ("big", b), akeys)
                        ri = nxt("cb", 2)
                        S.op("act", lambda a, b=b, ri=ri: a.activation(out=scr[:, ri, :T], in_=big[b][:, :T], func=AF.Relu),
                             reads=[("big", b)], writes=[("scr", ri)])
                        S.op("dve", lambda v, ri=ri, sbk=sbk, j=j: v.tensor_mul(out=mixT[:, sbk * 4 + j, :T], in0=scr[:, ri, :T], in1=scr[:, ri, :T]),
                             reads=[("scr", ri)], writes=[("mixT", sbk * 4 + j)])
                        if extra:
                            b2 = nbig()
                            gemm_fm(s, j * 128, 128, (0, 32), big[b2][:, :32], ("big", b2), ["actS"], src_=actS)
                            r2 = 2 + nxt("cb2", 2)
                            S.op("act", lambda a, b2=b2, r2=r2: a.activation(out=scr[:, r2, :32], in_=big[b2][:, :32], func=AF.Relu),
                                 reads=[("big", b2)], writes=[("scr", r2)])
                            S.op("dve", lambda v, r2=r2, sbk=sbk, j=j: v.tensor_mul(out=hidS[:, sbk * 4 + j, :], in0=scr[:, r2, :32], in1=scr[:, r2, :32]),
                                 reads=[("scr", r2)], writes=[("hidS", sbk * 4 + j)])
                for cb in range(4):
                    s = ring_load([(w_down_v, qf * 16, cb * 512, 512, 0)])
                    for mi, (off, n) in enumerate(mtiles):
                        b = nbig()
                        gemm_tm(mixT, mkeys, s, 0, 512, off, n, big[b][:n, :], ("big", b))
                        S.op("dve", lambda v, mi=mi, n=n, cb=cb, b=b: v.tensor_add(out=hbuf[:n, mi, cb * 512:(cb + 1) * 512],
                                                                                  in0=hbuf[:n, mi, cb * 512:(cb + 1) * 512], in1=big[b][:n, :]),
                             reads=[("h", mi, cb), ("big", b)], writes=[("h", mi, cb)])
                        if qf == 3 and cb == 3:
                            final_norm_store(mi, n, out_y[off:off + n, :])
                    if extra:
                        def sfn(t, s=s, cb=cb, qf=qf):
                            last = None
                            for k in range(KT):
                                last = t.matmul(big[4 + cb][:32, :], lhsT=hidS[:, k, 0:32], rhs=ring[s][:, k, :],
                                                start=(qf == 0 and k == 0), stop=(qf == 3 and k == KT - 1))
                            return last
                        S.op("pe", sfn, reads=[("ring", s)] + hkeys, writes=[("big", 4 + cb)])

            if extra:
                S.op("sp", lambda q: [q.dma_start(out=hbuf[:32, 0, :], in_=hs_scr)], reads=["hs_scr"],
                     writes=[("h", 0, cb) for cb in range(4)], dma=("h", 0))
                for cb in range(4):
                    S.op("dve", lambda v, cb=cb: v.tensor_add(out=hbuf[:32, 0, cb * 512:(cb + 1) * 512], in0=hbuf[:32, 0, cb * 512:(cb + 1) * 512],
                                                             in1=big[4 + cb][:32, :]),
                         reads=[("h", 0, cb), ("big", 4 + cb)], writes=[("h", 0, cb)])
                final_norm_store(0, 32, ys)

        def store_conv_tail(nrows_src, src_lo):
            pflat = big[7]
            for ct in range(8):
                q4 = ct % 4
                S.op("pe", lambda t, ct=ct, q4=q4: t.transpose(out=pflat[:nrows_src, q4 * 128:(q4 + 1) * 128],
                                                               in_=utail[:, ct, src_lo:src_lo + nrows_src], identity=identf[:]),
                     reads=[("utail", ct), "identf"], writes=[("big", 7)])
                S.op("dve", lambda v, ct=ct, q4=q4: v.tensor_copy(out=tok1k[:nrows_src, ct * 128:(ct + 1) * 128],
                                                                 in_=pflat[:nrows_src, q4 * 128:(q4 + 1) * 128]),
                     reads=[("big", 7)], writes=[("tok1k", ct), ("xn", 1)])

        pm = [(i * 128, 128) for i in range(4)]
        if sample:
            for j in range(2):
                S.op("sp", lambda q, j=j: [q.dma_start(out=Sst[j][:], in_=sh[j].rearrange("h d e -> d h e"))],
                     writes=[("S", j, h) for h in range(NH)], dma=("shl", j))
                S.op("dve", lambda v, j=j: v.tensor_copy(out=Sbf[j][:], in_=Sst[j][:]), reads=[("S", j, h) for h in range(NH)],
                     writes=[("Sbf", j, h) for h in range(NH)])
                S.op("sp", lambda q, j=j: [q.dma_start(out=tok1k[0:30, :], in_=sc[j])], writes=[("tok1k", ct) for ct in range(8)] + [("xn", 1)], dma="scl")
                for ct in range(8):
                    S.op("pe", lambda t, ct=ct: t.transpose(out=big[6][:, ct * 32:ct * 32 + 30], in_=tok1k[0:30, ct * 128:(ct + 1) * 128], identity=identf[0:30, 0:30]),
                         reads=[("tok1k", ct), ("xn", 1), "identf"], writes=[("big", 6)])
                    S.op("dve", lambda v, ct=ct, j=j: v.tensor_copy(out=ubs[:, ct, j, 0:30], in_=big[6][:, ct * 32:ct * 32 + 30]),
                         reads=[("big", 6)], writes=[("us", ct, j)])
                S.op("sp", lambda q, j=j: [q.dma_start(out=ncs[j, 0:14, :], in_=sc[j, 16:30, :])], dma=("ncs0", j))
            process_tb("sample", xs, 32, [(0, 16), (16, 16)], [0, 1], part_a=True)
            for j in range(2):
                S.op("sp", lambda q, j=j: [q.dma_start(out=nhs[j].rearrange("h d e -> d h e"), in_=Sst[j][:])],
                     reads=[("S", j, h) for h in range(NH)], dma=("nhs", j))
                store_conv_tail(16, j * 16)
                S.op("sp", lambda q, j=j: [q.dma_start(out=ncs[j, 14:30, :], in_=tok1k[0:16, :])], reads=[("tok1k", ct) for ct in range(8)] + [("xn", 1)],
                     dma=("ncs1", j))
        S.op("dve", lambda v: v.memset(Sst[0][:], 0.0), writes=[("S", 0, h) for h in range(NH)])
        S.op("dve", lambda v: v.memset(Sbf[0][:], 0.0), writes=[("Sbf", 0, h) for h in range(NH)])
        for tb in range(n_prev):
            process_tb("prev_last" if tb == n_prev - 1 else "prev", xp[tb * TW:(tb + 1) * TW, :], TW, pm, [0, 0, 0, 0])
        for tb in range(n_main):
            last = tb == n_main - 1
            process_tb("main", xm[tb * TW:(tb + 1) * TW, :], TW, pm, [0, 0, 0, 0], out_y=yp[tb * TW:(tb + 1) * TW, :],
                       last_main=last, extra=(last and sample))
        S.op("sp", lambda q: [q.dma_start(out=nhp.rearrange("h d e -> d h e"), in_=Sst[0][:])], reads=[("S", 0, h) for h in range(NH)], dma="nhp")
        store_conv_tail(32, 0)
        S.op("sp", lambda q: [q.dma_start(out=ncp[:, :], in_=tok1k[2:32, :])], reads=[("tok1k", ct) for ct in range(8)] + [("xn", 1)], dma="ncp")
        S.emit()
    return nc


def _pack_pvec(norm_mix_g, norm_mlp_g, b_dw, ln_g, ln_b, lb_logits, w_dw):
    pv = np.zeros((128, NPV), np.float32)
    pv[:, 0:16] = norm_mix_g.reshape(16, 128).T
    pv[:, 16:32] = norm_mlp_g.reshape(16, 128).T
    pv[:, 32:40] = b_dw.reshape(8, 128).T
    pv[:, 40:48] = ln_g.reshape(8, 128).T
    pv[:, 48:56] = ln_b.reshape(8, 128).T
    pv[:, 56:64] = lb_logits[0].reshape(8, 128).T
    pv[:, 64:72] = lb_logits[1].reshape(8, 128).T
    pv[:, 72:72 + 248] = w_dw.T.reshape(8, 128, CW).transpose(1, 0, 2).reshape(128, 248)
    return pv


_NC_CACHE = {}


def kernel(x_prompt, x_sample, state_conv, state_hgrn, norm_mix_g, w_in, w_dw, b_dw, ln_conv_g, ln_conv_b,
           lb_logits, hgrn_norm_g, w_out, norm_mlp_g, w_up, w_down, norm_final_g):
    f = lambda a: np.ascontiguousarray(np.asarray(a, dtype=np.float32))
    x_prompt, x_sample, state_conv, state_hgrn = f(x_prompt), f(x_sample), f(state_conv), f(state_hgrn)
    B, SEQ, _ = x_prompt.shape
    half = SEQ // 2
    pv = _pack_pvec(f(norm_mix_g)[0], f(norm_mlp_g)[0], f(b_dw)[0], f(ln_conv_g)[0], f(ln_conv_b)[0], f(lb_logits), f(w_dw)[0])
    w_in0 = f(w_in)[0]
    cols = []
    for p in range(4):
        cols += list(range(p * 256, (p + 1) * 256)) + list(range(1024 + p * 256, 1024 + (p + 1) * 256))
    for h in range(NH):
        for i in range(4):
            cols += list(range(2048 + i * 1024 + h * 128, 2048 + i * 1024 + (h + 1) * 128))
    w_in_hm = np.ascontiguousarray(w_in0[:, np.asarray(cols)])
    shared = {"w_in": w_in0, "w_in_hm": w_in_hm, "w_out": f(w_out)[0], "w_up": f(w_up)[0], "w_down": f(w_down)[0], "pvec": pv,
              "gfin": f(norm_final_g), "hg": f(hgrn_norm_g)[0]}
    zeros_half = np.zeros((half, D), np.float32)
    in_maps = []
    for c in range(8):
        s, hf = c // 2, c % 2
        m = dict(shared)
        m["xm"] = np.ascontiguousarray(x_prompt[s, hf * half:(hf + 1) * half])
        m["xp"] = np.ascontiguousarray(x_prompt[s, 0:half]) if hf == 1 else zeros_half
        m["xs"] = np.ascontiguousarray(x_sample[2 * c:2 * c + 2].reshape(32, D))
        m["sc"] = np.ascontiguousarray(state_conv[0, 2 * c:2 * c + 2])
        m["sh"] = np.ascontiguousarray(state_hgrn[0, 2 * c:2 * c + 2])
        in_maps.append(m)
    if "nc" not in _NC_CACHE:
        _NC_CACHE["nc"] = build()
    res = run_bass_kernel_spmd(_NC_CACHE["nc"], in_maps, core_ids=list(range(8)))
    r = res.results
    y_prompt = np.zeros((B, SEQ, D), np.float32)
    y_sample = np.zeros((16, 16, D), np.float32)
    ncp = np.zeros((1, B, 30, CONV), np.float32)
    nhp = np.zeros((1, B, NH, 128, 128), np.float32)
    ncs = np.zeros((1, 16, 30, CONV), np.float32)
    nhs = np.zeros((1, 16, NH, 128, 128), np.float32)
    for c in range(8):
        s, hf = c // 2, c % 2
        y_prompt[s, hf * half:(hf + 1) * half] = r[c]["yp"]
        y_sample[2 * c:2 * c + 2] = r[c]["ys"].reshape(2, 16, D)
        ncs[0, 2 * c:2 * c + 2] = r[c]["ncs"]
        nhs[0, 2 * c:2 * c + 2] = r[c]["nhs"]
        if hf == 1:
            ncp[0, s] = r[c]["ncp"]
            nhp[0, s] = r[c]["nhp"]
    return (y_prompt, y_sample, ncp, nhp, ncs, nhs)
```

```python
import contextlib
import numpy as np
import concourse.bass as bass
import concourse.mybir as mybir
from concourse.bass_utils import run_bass_kernel_spmd

F32 = mybir.dt.float32
BF16 = mybir.dt.bfloat16
AF = mybir.ActivationFunctionType
ALU = mybir.AluOpType

D = 2048
KT = 16
CONV = 1024
NH = 8
DFF = 8192
IN_COLS = 6144
CW = 31
EPS = 1e-6
TW = 512
NPV = 72 + 248


class Op:
    __slots__ = ("eng", "fn", "deps", "sem", "val", "know", "flag", "is_dma", "waits")


class Sched:
    ENGS = ("pe", "act", "dve", "pool", "sp")
    EPOCH = 8000

    def __init__(self, nc):
        self.nc = nc
        self.all = []
        self.last_w = {}
        self.readers = {}
        self.dma_last = {}
        self.dma_count = {}

    def capture(self):
        self._cap = []

    def end_capture(self):
        lst = self._cap
        self._cap = None
        return lst

    def replay(self, *lists, span=None):
        items = []
        for li, lst in enumerate(lists):
            lo, hi = (span[li] if span else (0.0, 1.0))
            for i, a in enumerate(lst):
                items.append((lo + (hi - lo) * (i + 0.5) / len(lst), li, i, a))
        items.sort(key=lambda x: (x[0], x[1], x[2]))
        for _, _, _, a in items:
            self.op(*a)

    def op(self, eng, fn, reads=(), writes=(), dma=None, ndma=1):
        if getattr(self, "_cap", None) is not None:
            self._cap.append((eng, fn, tuple(reads), tuple(writes), dma, ndma))
            return None
        o = Op()
        o.eng = eng
        o.fn = fn
        o.deps = set()
        o.is_dma = dma is not None
        o.flag = False
        o.know = None
        for k in reads:
            w = self.last_w.get(k)
            if w is not None:
                o.deps.add(w)
            if isinstance(k, tuple) and k[0] == "big":
                for r in self.readers.get(k, ()):
                    if r.eng != eng:
                        o.deps.add(r)
        for k in writes:
            w = self.last_w.get(k)
            if w is not None:
                o.deps.add(w)
            for r in self.readers.get(k, ()):
                o.deps.add(r)
        for k in reads:
            self.readers.setdefault(k, []).append(o)
        for k in writes:
            self.last_w[k] = o
            self.readers[k] = []
        if o.is_dma:
            p = self.dma_last.get(dma)
            if p is not None:
                o.deps.add(p)
            self.dma_last[dma] = o
            c = self.dma_count.get(dma, 0) + ndma
            self.dma_count[dma] = c
            o.sem = ("dma", dma)
            o.val = 16 * c
            o.flag = True
        o.deps.discard(o)
        self.all.append(o)
        return o

    def emit(self):
        nc = self.nc
        for o in self.all:
            for d in o.deps:
                if o.eng == "pe" and d.eng == "pe" and not d.is_dma:
                    continue
                d.flag = True
        cnt = {e: 0 for e in self.ENGS}
        for o in self.all:
            if o.is_dma:
                continue
            if o.flag:
                c = cnt[o.eng]
                o.sem = ("eng", o.eng, c // self.EPOCH)
                o.val = (c % self.EPOCH) + 1
                cnt[o.eng] = c + 1
        know = {e: {} for e in self.ENGS}
        for o in self.all:
            kn = know[o.eng]
            wm = {}
            dd = [d for d in o.deps if not (o.eng == "pe" and d.eng == "pe" and not d.is_dma)]
            for d in sorted(dd, key=lambda d: (str(d.sem), d.val)):
                if kn.get(d.sem, 0) < d.val:
                    wm[d.sem] = max(wm.get(d.sem, 0), d.val)
                    kn[d.sem] = d.val
                    if d.know:
                        for s, v in d.know.items():
                            if kn.get(s, 0) < v:
                                kn[s] = v
            o.waits = list(wm.items())
            if o.flag:
                o.know = dict(kn)
        final = [(("dma", k), 16 * c) for k, c in self.dma_count.items()]
        keys = []
        seen = set()
        for o in self.all:
            if o.flag and o.sem not in seen:
                seen.add(o.sem)
                keys.append(o.sem)
        self.nsem = len(keys)
        with contextlib.ExitStack() as st:
            sems = {}
            for i, k in enumerate(keys):
                sems[k] = st.enter_context(nc.semaphore("s%d" % i))
            block = st.enter_context(nc.Block())
            engobj = {"pe": "tensor", "act": "scalar", "dve": "vector", "pool": "gpsimd", "sp": "sync"}
            for e in self.ENGS:
                ops = [o for o in self.all if o.eng == e]
                fin = final if e == "sp" else []
                if not ops and not fin:
                    continue

                def body(eng, ops=ops, fin=fin):
                    for o in ops:
                        for s, v in o.waits:
                            eng.wait_ge(sems[s], v)
                        ins = o.fn(eng)
                        if o.flag:
                            if o.is_dma:
                                for i_ in ins:
                                    i_.then_inc(sems[o.sem], 16)
                            else:
                                ins.then_inc(sems[o.sem], 1)
                    for s, v in fin:
                        eng.wait_ge(sems[s], v)

                getattr(block, engobj[e])(body)


def build(n_prev=4, n_main=4, sample=True, ring_slots=3):
    nc = bass.Bass("TRN2", target_bir_lowering=False)
    dt_in = lambda n, sh: nc.dram_tensor(n, sh, F32, kind="ExternalInput").ap()
    dt_out = lambda n, sh: nc.dram_tensor(n, sh, F32, kind="ExternalOutput").ap()
    xm = dt_in("xm", [n_main * TW, D])
    xp = dt_in("xp", [max(n_prev, 1) * TW, D])
    xs = dt_in("xs", [32, D])
    sc = dt_in("sc", [2, 30, CONV])
    sh = dt_in("sh", [2, NH, 128, 128])
    w_in = dt_in("w_in", [D, IN_COLS])
    w_in_hm = dt_in("w_in_hm", [D, IN_COLS])
    w_out = dt_in("w_out", [D, D])
    w_up = dt_in("w_up", [D, DFF])
    w_down = dt_in("w_down", [DFF, D])
    pvec_d = dt_in("pvec", [128, NPV])
    gfin_d = dt_in("gfin", [D])
    hg_d = dt_in("hg", [CONV])
    yp = dt_out("yp", [n_main * TW, D])
    ys = dt_out("ys", [32, D])
    ncp = dt_out("ncp", [30, CONV])
    nhp = dt_out("nhp", [NH, 128, 128])
    ncs = dt_out("ncs", [2, 30, CONV])
    nhs = dt_out("nhs", [2, NH, 128, 128])
    hs_scr = nc.dram_tensor("hs_scr", [32, D], F32, kind="Internal").ap()

    w_in_v = w_in.rearrange("(kt p) c -> p kt c", p=128)
    w_hm_v = w_in_hm.rearrange("(kt p) c -> p kt c", p=128)
    w_out_v = w_out.rearrange("(kt p) c -> p kt c", p=128)
    w_up_v = w_up.rearrange("(kt p) c -> p kt c", p=128)
    w_down_v = w_down.rearrange("(kt p) c -> p kt c", p=128)

    st = contextlib.ExitStack()
    with st:
        sb = lambda n, shp, dt: st.enter_context(nc.sbuf_tensor(n, shp, dt))
        ps = lambda n, shp, dt: st.enter_context(nc.psum_tensor(n, shp, dt))
        NS = ring_slots
        ring = [sb("ring%d" % i, [128, KT, 512], BF16) for i in range(NS)]
        hbuf = sb("hbuf", [128, 4, D], F32)
        xn = [sb("xn%d" % i, [128, D], BF16) for i in range(2)]
        actT = sb("actT", [128, KT, TW], BF16)
        tok1k = xn[1][0:32, :].bitcast(F32)
        mixT = sb("mixT", [128, KT, TW], BF16)
        gfin = sb("gfin_r", [128, D], F32)
        hgr = sb("hg_r", [128, CONV], F32)
        pvec = sb("pvec_s", [128, NPV], F32)
        ident = sb("ident", [128, 128], BF16)
        identf = sb("identf", [128, 128], F32)
        maskT = sb("maskT", [128, 128], F32)
        zeros = sb("zeros", [128, 128], F32)
        zeros4 = zeros[:, 0:1].to_broadcast([128, TW])
        ones = sb("ones", [128, 128], BF16)
        lbt = sb("lbt", [128, 4, NH], F32)
        dg = sb("dg", [128, CW, 128], BF16)
        ubuf = sb("ubuf", [128, 8, 30 + TW], BF16)
        ubs = sb("ubs", [128, 8, 2, 46], BF16)
        scr = sb("scr", [128, 8, 512], F32)
        cbf = [sb("cbf%d" % i, [128, TW], BF16) for i in range(2)]
        csq = [sb("csq%d" % i, [128, TW], BF16) for i in range(2)]
        cmu = sb("cmu", [128, TW], F32)
        crs = sb("crs", [128, TW], F32)
        tmpE2 = [sb("tmpE%d" % i, [128, TW], F32) for i in range(2)]
        utail = sb("utail", [128, 8, 32], F32)
        Sst = [sb("S%d" % i, [128, NH, 128], F32) for i in range(2)]
        Sbf = [sb("Sbf%d" % i, [128, NH, 128], BF16) for i in range(2)]
        vtok = [sb("vtok%d" % i, [128, 128], BF16) for i in range(2)]
        ktok = [sb("ktok%d" % i, [128, 128], BF16) for i in range(2)]
        attT = [sb("attT%d" % i, [128, 128], BF16) for i in range(2)]
        btok = [sb("btok%d" % i, [128, 128], BF16) for i in range(2)]
        gt = [sb("gt%d" % i, [128, 128], F32) for i in range(2)]
        jsm = sb("jsm", [128, 128], BF16)
        kq = [sb("kq%d" % i, [128, 3 * TW], BF16) for i in range(2)]
        actS = sb("actS", [128, KT, 32], BF16)
        hidS = sb("hidS", [128, KT, 32], BF16)
        pend = [sb("pend%d" % i, [128, 4], F32) for i in range(2)]
        sfx = [sb("sfx%d" % i, [128, 4], F32) for i in range(2)]
        stt = sb("stt", [128, 64], F32)

        big = [ps("bank%d" % i, [128, 512], F32) for i in range(8)]
        bbf = lambda b_: big[b_][:, :].bitcast(BF16)
        cur = {"big": [0, 1, 2], "i": 0}

        def nbig():
            lst = cur["big"]
            v = lst[cur["i"] % len(lst)]
            cur["i"] += 1
            return v
        TN = [3, 4]
        TH = [5, 6]
        IGB = [3, 4]
        HB = [2, 7]

        S = Sched(nc)
        cnt = {"ring": 0, "big": 0, "pT": 0, "alt": 0, "ig": 0, "xn": 0, "cb": 0, "stt": 0, "cb2": 0}

        def nxt(name, mod):
            v = cnt[name]
            cnt[name] = (v + 1) % mod if mod else v + 1
            return v

        gmixT = lambda k: k
        gmlpT = lambda k: 16 + k
        bdwT = lambda ct: pvec[:, 32 + ct:33 + ct]
        lngT = lambda ct: pvec[:, 40 + ct:41 + ct]
        lnbT = lambda ct: pvec[:, 48 + ct:49 + ct]
        wdwT = lambda ct, j: pvec[:, 72 + ct * CW + j:73 + ct * CW + j]
        lbv = lambda h: lbt[:, 1, h:h + 1]
        omlv = lambda h: lbt[:, 2, h:h + 1]
        nomlv = lambda h: lbt[:, 3, h:h + 1]

        S.op("sp", lambda q: [q.dma_start(out=pvec[:], in_=pvec_d)], writes=["pvec"], dma="pvec")
        S.op("sp", lambda q: [q.dma_start(out=gfin[:], in_=gfin_d.partition_broadcast(128))], writes=["gfin"], dma="gfin")
        S.op("sp", lambda q: [q.dma_start(out=hgr[:], in_=hg_d.partition_broadcast(128))], writes=["hgr"], dma="hgr")
        S.op("pool", lambda g: g.memset(identf[:], 1.0), writes=["identf"])
        S.op("pool", lambda g: g.affine_select(out=identf[:], in_=identf[:], pattern=[[1, 128]], compare_op=ALU.is_equal,
                                               fill=0.0, base=0, channel_multiplier=-1), reads=["identf"], writes=["identf"])
        S.op("pool", lambda g: g.memset(maskT[:], 1.0), writes=["maskT"])
        S.op("pool", lambda g: g.affine_select(out=maskT[:], in_=maskT[:], pattern=[[1, 128]], compare_op=ALU.is_ge,
                                               fill=0.0, base=0, channel_multiplier=-1), reads=["maskT"], writes=["maskT"])
        S.op("dve", lambda v: v.tensor_copy(out=ident[:], in_=identf[:]), reads=["identf"], writes=["ident"])
        S.op("dve", lambda v: v.memset(zeros[:], 0.0), writes=["zeros"])

        S.op("dve", lambda v: v.memset(ones[:], 1.0), writes=["ones"])
        S.op("dve", lambda v: v.memset(ubuf[:, :, 0:30], 0.0), writes=[("u", ct) for ct in range(8)])
        S.op("dve", lambda v: v.tensor_sub(out=lbt[:, 0, :], in0=pvec[:, 64:72], in1=pvec[:, 56:64]), reads=["pvec"], writes=["lb0"])
        S.op("act", lambda a: a.activation(out=lbt[:, 0, :], in_=lbt[:, 0, :], func=AF.Exp), reads=["lb0"], writes=["lb0"])
        S.op("dve", lambda v: v.tensor_scalar_add(out=lbt[:, 1, :], in0=lbt[:, 0, :], scalar1=1.0), reads=["lb0"], writes=["lb1"])
        S.op("dve", lambda v: v.reciprocal(out=lbt[:, 1, :], in_=lbt[:, 1, :]), reads=["lb1"], writes=["lb1"])
        S.op("dve", lambda v: v.tensor_mul(out=lbt[:, 2, :], in0=lbt[:, 0, :], in1=lbt[:, 1, :]), reads=["lb0", "lb1"], writes=["lb2"])
        S.op("dve", lambda v: v.tensor_scalar_mul(out=lbt[:, 3, :], in0=lbt[:, 2, :], scalar1=-1.0), reads=["lb2"], writes=["lb3"])
        LBK = ["lb1", "lb2", "lb3"]
        nln = sb("nln", [128, 16], F32)
        S.op("dve", lambda v: v.tensor_scalar_mul(out=nln[:], in0=pvec[:, 40:56], scalar1=-1.0), reads=["pvec"], writes=["nln"])
        nlngT = lambda ct: nln[:, ct:ct + 1]
        nlnbT = lambda ct: nln[:, 8 + ct:9 + ct]
        PH0 = [("pH", 0), ("pHo", 0), ("pHs", 0)]
        PH1 = [("pH", 1), ("pHo", 1), ("pHs", 1)]

        def sk(g):
            return [("scr", g)]

        def ring_load(pieces):
            s = nxt("ring", NS)

            def fn(g, s=s, pieces=pieces):
                out = []
                for (view, kt0, c0, n, sc0) in pieces:
                    out.append(g.dma_start(out=ring[s][:, :, sc0:sc0 + n], in_=view[:, kt0:kt0 + KT, c0:c0 + n]))
                return out
            S.op("pool", fn, writes=[("ring", s)], dma=("ring", s), ndma=len(pieces))
            return s

        def gemm_fm(s, c0, ncols, acols, psap, pskey, akeys, src_=None):
            a0, a1 = acols
            src_ = actT if src_ is None else src_

            def fn(t):
                last = None
                for k in range(KT):
                    last = t.matmul(psap, lhsT=ring[s][:, k, c0:c0 + ncols], rhs=src_[:, k, a0:a1], start=(k == 0), stop=(k == KT - 1))
                return last
            S.op("pe", fn, reads=[("ring", s)] + akeys, writes=[pskey])

        def gemm_tm(src, srckeys, s, c0, ncols, off, n, psap, pskey):
            def fn(t):
                last = None
                for k in range(KT):
                    last = t.matmul(psap, lhsT=src[:, k, off:off + n], rhs=ring[s][:, k, c0:c0 + ncols], start=(k == 0), stop=(k == KT - 1))
                return last
            S.op("pe", fn, reads=[("ring", s)] + srckeys, writes=[pskey])

        def rsqrt_stat(col, n, scale):
            S.op("act", lambda a: a.activation(out=stt[:n, col + 1:col + 2], in_=stt[:n, col:col + 1], func=AF.Ln, scale=scale, bias=EPS),
                 reads=[("stt", col)], writes=[("stt", col + 1)])
            S.op("act", lambda a: a.activation(out=stt[:n, col + 1:col + 2], in_=stt[:n, col + 1:col + 2], func=AF.Exp, scale=-0.5),
                 reads=[("stt", col + 1)], writes=[("stt", col + 1)])

        def sigmoid_from(psap, pskey, outap, outkey, shape_n):
            okl = outkey if isinstance(outkey, list) else [outkey]
            S.op("act", lambda a: a.activation(out=outap, in_=psap, func=AF.Exp, scale=-1.0), reads=[pskey], writes=okl)
            S.op("act", lambda a: a.activation(out=outap, in_=outap, func=AF.Ln, bias=1.0), reads=okl, writes=okl)
            S.op("act", lambda a: a.activation(out=outap, in_=outap, func=AF.Exp, scale=-1.0), reads=okl, writes=okl)

        def norm_to_actT(mtiles, gfun, only=None):
            for mi, (off, n) in enumerate(mtiles):
                if only is not None and mi != only:
                    continue
                hk = [("h", mi, cb) for cb in range(4)]
                xi = nxt("xn", 2)
                sc_ = 2 * nxt("stt", 32)
                S.op("act", lambda a, mi=mi, n=n, xi=xi, sc_=sc_: a.activation(out=xn[xi][:n, :], in_=hbuf[:n, mi, :], func=AF.Square,
                                                                         accum_out=stt[:n, sc_:sc_ + 1]),
                     reads=hk, writes=[("xn", xi), ("stt", sc_)])
                rsqrt_stat(sc_, n, 1.0 / D)
                S.op("dve", lambda v, mi=mi, n=n, xi=xi, sc_=sc_: v.tensor_scalar_mul(out=xn[xi][:n, :], in0=hbuf[:n, mi, :],
                                                                               scalar1=stt[:n, sc_ + 1:sc_ + 2]),
                     reads=hk + [("stt", sc_ + 1)], writes=[("xn", xi)])
                for kg in range(4):
                    hf = TN[nxt("pT", 2)]

                    def tfn(t, n=n, xi=xi, kg=kg, hf=hf):
                        last = None
                        for j in range(4):
                            k = kg * 4 + j
                            last = t.transpose(out=bbf(hf)[:, j * 128:j * 128 + n], in_=xn[xi][:n, k * 128:(k + 1) * 128], identity=ident[:n, :n])
                        return last
                    S.op("pe", tfn, reads=[("xn", xi), "ident"], writes=[("big", hf)])
                    gsl = gfun(kg * 4)
                    g4 = pvec[:, gsl:gsl + 4].unsqueeze(2).to_broadcast([128, 4, n])
                    S.op("dve", lambda v, n=n, off=off, kg=kg, hf=hf, g4=g4: v.tensor_tensor(
                        out=actT[:, kg * 4:kg * 4 + 4, off:off + n],
                        in0=bbf(hf)[:, 0:512].rearrange("p (j c) -> p j c", c=128)[:, :, 0:n], in1=g4, op=ALU.mult),
                        reads=[("big", hf), "pvec"], writes=[("actT", mi, kg * 4 + j) for j in range(4)])

        def process_tb(mode, xsrc, T, mtiles, sidx, out_y=None, last_main=False, part_a=False, extra=False, next_x=None, preloaded=False):
            nm = len(mtiles)
            akeys = [("actT", mi, k) for mi in range(nm) for k in range(KT)]
            full = mode in ("main", "sample")
            if not preloaded:
                for mi, (off, n) in enumerate(mtiles):
                    S.op("sp", lambda q, mi=mi, off=off, n=n: [q.dma_start(out=hbuf[:n, mi, :], in_=xsrc[off:off + n, :])],
                         writes=[("h", mi, cb) for cb in range(4)], dma=("h", mi))
            norm_to_actT(mtiles, gmixT)

            cur["big"] = [0, 1, 2]
            if full or mode == "prev_last":
                cur["big"] = [0, 1, 2, 3, 4, 5] if full else [0, 1, 2]
                if mode == "prev_last":
                    ccols = (T - 32, T)
                else:
                    ccols = (0, T)
                cn = ccols[1] - ccols[0]
                slots = {}

                def glu(ct):
                    p, jj = divmod(ct, 2)
                    if jj == 0:
                        slots[p] = ring_load([(w_hm_v, 0, p * 512, 512, 0)])
                    s = slots[p]
                    ba = nbig()
                    gemm_fm(s, jj * 128, 128, ccols, big[ba][:, :cn], ("big", ba), akeys)
                    bg = nbig()
                    gemm_fm(s, 256 + jj * 128, 128, ccols, big[bg][:, :cn], ("big", bg), akeys)
                    tE = scr[:, ct, :]
                    tk = sk(ct)
                    sigmoid_from(big[bg][:, :cn], ("big", bg), tE[:, :cn], tk, cn)
                    if mode == "prev_last":
                        S.op("dve", lambda v: v.tensor_mul(out=ubuf[:, ct, 0:30], in0=big[ba][:, 2:32], in1=tE[:, 2:32]),
                             reads=[("big", ba)] + tk, writes=[("u", ct)])
                    elif mode == "main":
                        S.op("dve", lambda v: v.tensor_mul(out=ubuf[:, ct, 30:30 + T], in0=big[ba][:, :T], in1=tE[:, :T]),
                             reads=[("big", ba)] + tk, writes=[("u", ct)])
                        if last_main:
                            S.op("dve", lambda v: v.tensor_mul(out=utail[:, ct, :], in0=big[ba][:, T - 32:T], in1=tE[:, T - 32:T]),
                                 reads=[("big", ba)] + tk, writes=[("utail", ct)])
                    else:
                        for si in range(2):
                            S.op("dve", lambda v, si=si: v.tensor_mul(out=ubs[:, ct, si, 30:46], in0=big[ba][:, si * 16:si * 16 + 16],
                                                                     in1=tE[:, si * 16:si * 16 + 16]),
                                 reads=[("big", ba)] + tk, writes=[("us", ct, si)])
                        S.op("dve", lambda v: v.tensor_mul(out=utail[:, ct, :], in0=big[ba][:, 0:32], in1=tE[:, 0:32]),
                             reads=[("big", ba)] + tk, writes=[("utail", ct)])

                def conv_diag(ct):
                    S.op("dve", lambda v: v.tensor_tensor(out=dg[:], in0=identf[:].unsqueeze(1).to_broadcast([128, CW, 128]),
                                                          in1=pvec[:, 72 + ct * CW:72 + (ct + 1) * CW].unsqueeze(2).to_broadcast([128, CW, 128]),
                                                          op=ALU.mult),
                         reads=["identf", "pvec"], writes=[("dg", j) for j in range(CW)])

                def conv_rest(ct):
                    bc = nbig()

                    def cfn(t):
                        last = None
                        if mode == "main":
                            for j in range(CW):
                                last = t.matmul(big[bc][:, :T], lhsT=dg[:, j, :], rhs=ubuf[:, ct, j:j + T], start=(j == 0), stop=(j == CW - 1))
                        else:
                            for si in range(2):
                                for j in range(CW):
                                    last = t.matmul(big[bc][:, si * 16:si * 16 + 16], lhsT=dg[:, j, :], rhs=ubs[:, ct, si, j:j + 16],
                                                    start=(j == 0), stop=(j == CW - 1))
                        return last
                    ukeys = [("u", ct)] if mode == "main" else [("us", ct, 0), ("us", ct, 1)]
                    S.op("pe", cfn, reads=ukeys + [("dg", j) for j in range(CW)], writes=[("big", bc)])
                    if mode == "main":
                        S.op("dve", lambda v: v.tensor_copy(out=ubuf[:, ct, 0:30], in_=ubuf[:, ct, T:T + 30]),
                             reads=[("u", ct)], writes=[("u", ct)])
                    S.op("act", lambda a_: a_.activation(out=scr[:, ct, :T], in_=big[bc][:, :T], func=AF.Identity, bias=bdwT(ct)),
                         reads=[("big", bc), "pvec"], writes=sk(ct))
                    ci = nxt("cb", 2)
                    S.op("act", lambda a_: a_.activation(out=csq[ci][:, :T], in_=big[bc][:, :T], func=AF.Square, bias=bdwT(ct)),
                         reads=[("big", bc), "pvec"], writes=[("csq", ci)])
                    S.op("dve", lambda v: v.tensor_copy(out=cbf[ci][:, :T], in_=scr[:, ct, :T]),
                         reads=sk(ct), writes=[("cbf", ci)])
                    return ci

                def conv_stats(ct, ci):
                    S.op("pe", lambda t: t.matmul(big[6][:, :T], lhsT=ones[:], rhs=cbf[ci][:, :T], start=(ct == 0), stop=(ct == 7)),
                         reads=[("cbf", ci), "ones"], writes=[("big", 6)])
                    S.op("pe", lambda t: t.matmul(big[7][:, :T], lhsT=ones[:], rhs=csq[ci][:, :T], start=(ct == 0), stop=(ct == 7)),
                         reads=[("csq", ci), "ones"], writes=[("big", 7)])

                glu(0)
                pend_stats = None
                for ct in range(8):
                    if mode != "prev_last":
                        conv_diag(ct)
                    if ct + 1 < 8:
                        glu(ct + 1)
                    if mode != "prev_last":
                        ci_ = conv_rest(ct)
                        if pend_stats is not None:
                            conv_stats(*pend_stats)
                        pend_stats = (ct, ci_)
                if pend_stats is not None:
                    conv_stats(*pend_stats)
                if full:
                    pXsq = big[7]
                    S.op("act", lambda a: a.activation(out=cmu[:, :T], in_=big[6][:, :T], func=AF.Copy, scale=1.0 / CONV),
                         reads=[("big", 6)], writes=["cmu"])
                    S.op("dve", lambda v: v.tensor_mul(out=crs[:, :T], in0=cmu[:, :T], in1=cmu[:, :T]), reads=["cmu"], writes=["crs"])
                    S.op("dve", lambda v: v.scalar_tensor_tensor(out=crs[:, :T], in0=pXsq[:, :T], scalar=1.0 / CONV, in1=crs[:, :T],
                                                                 op0=ALU.mult, op1=ALU.subtract),
                         reads=[("big", 7), "crs"], writes=["crs"])
                    S.op("act", lambda a: a.activation(out=crs[:, :T], in_=crs[:, :T], func=AF.Ln, bias=EPS), reads=["crs"], writes=["crs"])
                    S.op("act", lambda a: a.activation(out=crs[:, :T], in_=crs[:, :T], func=AF.Exp, scale=-0.5), reads=["crs"], writes=["crs"])
                    def ln_ct(ct):
                        tE_ = tmpE2[ct % 2]
                        tEk = ("tmpE", ct % 2)
                        S.op("dve", lambda v, ct=ct: v.tensor_sub(out=scr[:, ct, :T], in0=scr[:, ct, :T], in1=cmu[:, :T]),
                             reads=sk(ct) + ["cmu"], writes=sk(ct))
                        S.op("dve", lambda v, ct=ct: v.tensor_mul(out=scr[:, ct, :T], in0=scr[:, ct, :T], in1=crs[:, :T]),
                             reads=sk(ct) + ["crs"], writes=sk(ct))
                        S.op("act", lambda a, ct=ct, tE_=tE_: a.activation(out=tE_[:, :T], in_=scr[:, ct, :T], func=AF.Exp, scale=nlngT(ct), bias=nlnbT(ct)),
                             reads=sk(ct) + ["nln"], writes=[tEk])
                        S.op("act", lambda a, ct=ct: a.activation(out=scr[:, ct, :T], in_=scr[:, ct, :T], func=AF.Identity, scale=lngT(ct), bias=lnbT(ct)),
                             reads=sk(ct) + ["pvec"], writes=sk(ct))
                        S.op("act", lambda a, tE_=tE_: a.activation(out=tE_[:, :T], in_=tE_[:, :T], func=AF.Ln, bias=1.0), reads=[tEk], writes=[tEk])
                        S.op("act", lambda a, tE_=tE_: a.activation(out=tE_[:, :T], in_=tE_[:, :T], func=AF.Exp, scale=-1.0), reads=[tEk], writes=[tEk])
                        S.op("dve", lambda v, ct=ct, tE_=tE_: v.tensor_mul(out=mixT[:, ct, :T], in0=scr[:, ct, :T], in1=tE_[:, :T]),
                             reads=sk(ct) + [tEk], writes=[("mixT", ct)])

                    for ct in (4, 5, 6, 7):
                        ln_ct(ct)

            if not full:
                cur["big"] = [0, 1]
                X = lambda g: scr[:, g, :T]
                vall = mixT[:, :, :].rearrange("p a b -> p (a b)")
                kt4 = [scr[:, 6, :].bitcast(BF16), scr[:, 7, :].bitcast(BF16)]
                for half in range(2):
                    s_i = ring_load([(w_in_v, 0, 4096 + half * 512, 512, 0)])
                    for mi, (off, n) in enumerate(mtiles):
                        b = IGB[nxt("ig", 2)]
                        gemm_tm(actT, [("actT", mi, k) for k in range(KT)], s_i, 0, 512, off, n, big[b][:n, :], ("big", b))
                        S.op("act", lambda a, b=b, n=n, mi=mi, half=half: a.activation(
                            out=vall[:n, mi * 1024 + half * 512:mi * 1024 + half * 512 + 512], in_=big[b][:n, :], func=AF.Copy),
                            reads=[("big", b)], writes=[("mixT", mi * 2 + half)])
                fs = {}

                def fgemm(h):
                    half, j = divmod(h, 4)
                    if j == 0:
                        fs[half] = ring_load([(w_in_v, 0, 3072 + half * 512, 512, 0)])
                    bf_ = nbig()
                    gemm_fm(fs[half], j * 128, 128, (0, T), big[bf_][:, :T], ("big", bf_), akeys)
                    return bf_

                def pgates(h, bf_):
                    hp = h % 2
                    KE = kq[hp][:, TW:TW + T]
                    kek = ("KE", hp)
                    ga, gb, gc = 3 * hp, 3 * hp + 1, 3 * hp + 2
                    A, B, C = scr[:, ga, :T], scr[:, gb, :T], scr[:, gc, :T]
                    ka, kb, kc = ("scr", ga), ("scr", gb), ("scr", gc)
                    sigmoid_from(big[bf_][:, :T], ("big", bf_), A, ka, T)
                    S.op("act", lambda a: a.activation(out=B, in_=A, func=AF.Ln, scale=omlv(h), bias=lbv(h)),
                         reads=[ka] + LBK, writes=[kb])
                    S.op("dve", lambda v: v.tensor_scalar(out=C, in0=A, scalar1=nomlv(h), scalar2=omlv(h), op0=ALU.mult, op1=ALU.add),
                         reads=[ka] + LBK, writes=[kc])
                    S.op("dve", lambda v: v.tensor_tensor_scan(out=A, data0=B, data1=zeros[:, 0:1].to_broadcast([128, T]), initial=0.0,
                                                               op0=ALU.add, op1=ALU.add),
                         reads=[kb, "zeros"], writes=[ka])
                    S.op("act", lambda a: a.activation(out=B, in_=A, func=AF.Exp, scale=-1.0, bias=scr[:, ga, T - 1:T]),
                         reads=[ka], writes=[kb])
                    S.op("act", lambda a: a.activation(out=sfx[hp][:, 0:1], in_=scr[:, ga, T - 1:T], func=AF.Exp),
                         reads=[ka], writes=[("sfx", hp)])
                    S.op("dve", lambda v: v.tensor_mul(out=KE, in0=C, in1=B), reads=[kc, kb], writes=[kek])

                def pupdate(h):
                    hp = h % 2
                    KE = kq[hp][:, TW:TW + T]
                    kek = ("KE", hp)
                    al = nxt("alt", 2)
                    hb = HB[al]
                    hf = TH[nxt("pT", 2)]

                    def tfn(t):
                        last = None
                        for c, (off, n) in enumerate(mtiles):
                            last = t.transpose(out=bbf(hf)[:n, c * 128:(c + 1) * 128], in_=KE[:, off:off + n], identity=ident[:])
                        return last
                    S.op("pe", tfn, reads=[kek, "ident"], writes=[("big", hf)])
                    S.op("act", lambda a: a.activation(out=kt4[al][:, 0:512], in_=bbf(hf)[:, 0:512], func=AF.Copy),
                         reads=[("big", hf)], writes=sk(6 + al))

                    def sfn(t):
                        last = None
                        for c, (off, n) in enumerate(mtiles):
                            last = t.matmul(big[hb][:, 256:384], lhsT=kt4[al][:n, c * 128:(c + 1) * 128],
                                            rhs=vall[:n, c * 1024 + h * 128:c * 1024 + (h + 1) * 128], start=(c == 0), stop=(c == nm - 1))
                        return last
                    S.op("pe", sfn, reads=sk(6 + al) + [("mixT", c * 2 + h // 4) for c in range(nm)], writes=[("big", hb)])
                    S.op("dve", lambda v: v.scalar_tensor_tensor(
                        out=Sst[0][:, h, :], in0=Sst[0][:, h, :], scalar=sfx[hp][:, 0:1], in1=big[hb][:, 256:384],
                        op0=ALU.mult, op1=ALU.add),
                        reads=[("S", 0, h), ("sfx", hp), ("big", hb)], writes=[("S", 0, h)])
                    S.op("act", lambda a: a.activation(out=Sbf[0][:, h, :], in_=Sst[0][:, h, :], func=AF.Copy),
                         reads=[("S", 0, h)], writes=[("Sbf", 0, h)])

                fb = {0: fgemm(0)}
                pgates(0, fb[0])
                fb[1] = fgemm(1)
                for h in range(NH):
                    S.capture()
                    pupdate(h)
                    la = S.end_capture()
                    S.capture()
                    if h + 2 < NH:
                        fb[h + 2] = fgemm(h + 2)
                    if h + 1 < NH:
                        pgates(h + 1, fb[h + 1])
                    lb_ = S.end_capture()
                    if lb_:
                        S.replay(la, lb_)
                    else:
                        S.replay(la)
                return

            cur["big"] = [0, 1]
            X = lambda g: scr[:, g, :T]
            ncol = 256 if full else 128
            csz = mtiles[0][1]

            def head_gemm(h):
                s = ring_load([(w_hm_v, 0, 2048 + h * 512, 512, 0)])
                bf_ = nbig()
                gemm_fm(s, 128, 128, (0, T), big[bf_][:, :T], ("big", bf_), akeys)
                bq = None
                if full:
                    bq = nbig()
                    gemm_fm(s, 0, 128, (0, T), big[bq][:, :T], ("big", bq), akeys)
                return s, bf_, bq

            def head_gates(h, bf_, bq):
                hp = h % 2
                KD, KE, QD = kq[hp][:, 0:T], kq[hp][:, TW:TW + T], kq[hp][:, 2 * TW:2 * TW + T]
                kdk, kek, qdk = ("KD", hp), ("KE", hp), ("QD", hp)
                G = lambda i: scr[:, 4 + i, :T]
                gk = lambda i: ("scr", 4 + i)
                sigmoid_from(big[bf_][:, :T], ("big", bf_), G(0), gk(0), T)
                S.op("act", lambda a: a.activation(out=G(1), in_=G(0), func=AF.Ln, scale=omlv(h), bias=lbv(h)),
                     reads=[gk(0)] + LBK, writes=[gk(1)])
                S.op("dve", lambda v: v.tensor_scalar(out=G(2), in0=G(0), scalar1=nomlv(h), scalar2=omlv(h), op0=ALU.mult, op1=ALU.add),
                     reads=[gk(0)] + LBK, writes=[gk(2)])
                for mi, (off, n) in enumerate(mtiles):
                    S.op("dve", lambda v, off=off, n=n: v.tensor_tensor_scan(out=scr[:, 7, off:off + n], data0=scr[:, 5, off:off + n],
                                                                             data1=zeros[:, 0:n], initial=0.0, op0=ALU.add, op1=ALU.add),
                         reads=[gk(1), "zeros"], writes=[gk(3)])
                S.op("act", lambda a: a.activation(out=G(0), in_=G(3), func=AF.Exp, scale=-1.0), reads=[gk(3)], writes=[gk(0)])
                S.op("act", lambda a: a.activation(out=G(3), in_=G(3), func=AF.Exp), reads=[gk(3)], writes=[gk(3)])
                S.op("dve", lambda v: v.tensor_copy(out=pend[hp][:, 0:nm], in_=scr[:, 7, csz - 1:T:csz]), reads=[gk(3)], writes=[("pend", hp)])
                S.op("dve", lambda v: v.tensor_mul(out=KD, in0=G(2), in1=G(0)), reads=[gk(2), gk(0)], writes=[kdk])
                S.op("dve", lambda v: v.tensor_tensor(out=KE.rearrange("p (c n) -> p c n", n=csz), in0=KD.rearrange("p (c n) -> p c n", n=csz),
                                                      in1=pend[hp][:, 0:nm].unsqueeze(2).to_broadcast([128, nm, csz]), op=ALU.mult),
                     reads=[kdk, ("pend", hp)], writes=[kek])
                if full:
                    sigmoid_from(big[bq][:, :T], ("big", bq), G(1), gk(1), T)
                    S.op("dve", lambda v: v.tensor_mul(out=G(1), in0=big[bq][:, :T], in1=G(1)), reads=[("big", bq), gk(1)], writes=[gk(1)])
                    S.op("dve", lambda v: v.tensor_mul(out=QD, in0=G(1), in1=G(3)), reads=[gk(1), gk(3)], writes=[qdk])

            def chunk_s1(h, s, mi):
                off, n = mtiles[mi]
                hp = h % 2
                KD, KE, QD = kq[hp][:, 0:T], kq[hp][:, TW:TW + T], kq[hp][:, 2 * TW:2 * TW + T]
                kdk, kek, qdk = ("KD", hp), ("KE", hp), ("QD", hp)
                al = nxt("alt", 2)
                hb = HB[al]
                ig = IGB[nxt("ig", 2)]
                gemm_tm(actT, [("actT", mi, k) for k in range(KT)], s, 256, ncol, off, n, big[ig][:n, 0:ncol], ("big", ig))
                S.op("dve", lambda v: v.tensor_copy(out=vtok[al][:n, :], in_=big[ig][:n, 0:128]),
                     reads=[("big", ig)], writes=[("vtok", al)])
                if full:
                    sigmoid_from(big[ig][:n, 128:256], ("big", ig), gt[al][:n, :], ("gt", al), n)
                    S.op("dve", lambda v: v.tensor_mul(out=gt[al][:n, :], in0=big[ig][:n, 128:256], in1=gt[al][:n, :]),
                         reads=[("big", ig), ("gt", al)], writes=[("gt", al)])
                    S.op("dve", lambda v: v.tensor_mul(out=gt[al][:n, :], in0=gt[al][:n, :], in1=hgr[:n, h * 128:(h + 1) * 128]),
                         reads=[("gt", al), "hgr"], writes=[("gt", al)])
                hf = TH[nxt("pT", 2)]
                S.op("pe", lambda t: t.transpose(out=bbf(hf)[:n, 0:128], in_=KE[:, off:off + n], identity=ident[:]),
                     reads=[kek, "ident"], writes=[("big", hf)])
                S.op("act", lambda a: a.activation(out=ktok[al][:n, :], in_=bbf(hf)[:n, 0:128], func=AF.Copy),
                     reads=[("big", hf)], writes=[("ktok", al)])
                if full:
                    S.op("pe", lambda t: t.matmul(big[hb][:n, 0:n], lhsT=KD[:, off:off + n], rhs=QD[:, off:off + n], start=True, stop=True),
                         reads=[kdk, qdk], writes=[("big", hb)])
                    S.op("dve", lambda v: v.tensor_mul(out=attT[al][:n, :n], in0=big[hb][:n, 0:n], in1=maskT[:n, :n]),
                         reads=[("big", hb), "maskT"], writes=[("attT", al)])
                return al, hb

            def chunk_s2(h, mi, al, hb):
                off, n = mtiles[mi]
                si = sidx[mi]
                hp = h % 2
                QD = kq[hp][:, 2 * TW:2 * TW + T]
                qdk = ("QD", hp)
                if full:
                    def ofn(t):
                        t.matmul(big[hb][:n, 128:256], lhsT=attT[al][:n, :n], rhs=vtok[al][:n, :], start=True, stop=False)
                        return t.matmul(big[hb][:n, 128:256], lhsT=QD[:, off:off + n], rhs=Sbf[si][:, h, :], start=False, stop=True)
                    S.op("pe", ofn, reads=[("attT", al), ("vtok", al), qdk, ("Sbf", si, h)], writes=[("big", hb)])
                S.op("pe", lambda t: t.matmul(big[hb][:, 256:384], lhsT=ktok[al][:n, :], rhs=vtok[al][:n, :], start=True, stop=True),
                     reads=[("ktok", al), ("vtok", al)], writes=[("big", hb)])
                S.op("dve", lambda v: v.scalar_tensor_tensor(
                    out=Sst[si][:, h, :], in0=Sst[si][:, h, :], scalar=pend[hp][:, mi:mi + 1], in1=big[hb][:, 256:384],
                    op0=ALU.mult, op1=ALU.add),
                    reads=[("S", si, h), ("pend", hp), ("big", hb)], writes=[("S", si, h)])
                def sbf_copy():
                    S.op("act", lambda a: a.activation(out=Sbf[si][:, h, :], in_=Sst[si][:, h, :], func=AF.Copy),
                         reads=[("S", si, h)], writes=[("Sbf", si, h)])
                if not full:
                    sbf_copy()
                if full:
                    sc_ = 2 * nxt("stt", 32)
                    S.op("act", lambda a: a.activation(out=jsm[:n, :], in_=big[hb][:n, 128:256], func=AF.Square, accum_out=stt[:n, sc_:sc_ + 1]),
                         reads=[("big", hb)], writes=["jsm", ("stt", sc_)])
                    rsqrt_stat(sc_, n, 1.0 / 128)
                    S.op("dve", lambda v: v.scalar_tensor_tensor(
                        out=btok[al][:n, :], in0=big[hb][:n, 128:256], scalar=stt[:n, sc_ + 1:sc_ + 2], in1=gt[al][:n, :],
                        op0=ALU.mult, op1=ALU.mult),
                        reads=[("big", hb), ("stt", sc_ + 1), ("gt", al)], writes=[("btok", al)])
                    sbf_copy()

            def chunk_s2b(h, mi, al, hb):
                off, n = mtiles[mi]
                if full:
                    hf2 = TH[nxt("pT", 2)]
                    S.op("pe", lambda t: t.transpose(out=bbf(hf2)[:, 0:n], in_=btok[al][:n, :], identity=ident[:n, :n]),
                         reads=[("btok", al), "ident"], writes=[("big", hf2)])
                    S.op("dve", lambda v: v.tensor_copy(out=mixT[:, 8 + h, off:off + n], in_=bbf(hf2)[:, 0:n]),
                         reads=[("big", hf2)], writes=[("mixT", 8 + h)])

            def chunks(h, s):
                ctxs = {0: chunk_s1(h, s, 0)}
                for mi in range(nm):
                    if mi + 1 < nm:
                        ctxs[mi + 1] = chunk_s1(h, s, mi + 1)
                    if mi >= 1:
                        chunk_s2b(h, mi - 1, *ctxs[mi - 1])
                    chunk_s2(h, mi, *ctxs[mi])
                chunk_s2b(h, nm - 1, *ctxs[nm - 1])

            g = {0: head_gemm(0)}
            head_gates(0, g[0][1], g[0][2])
            if NH > 1:
                g[1] = head_gemm(1)
            for h in range(NH):
                S.capture()
                chunks(h, g[h][0])
                la = S.end_capture()
                S.capture()
                if h + 1 < NH:
                    head_gates(h + 1, g[h + 1][1], g[h + 1][2])
                if h + 2 < NH:
                    g[h + 2] = head_gemm(h + 2)
                lb_ = S.end_capture()
                lists = [la] + ([lb_] if lb_ else [])
                if h == 0:
                    S.capture()
                    for ct in (0, 1, 2, 3):
                        ln_ct(ct)
                    lists.append(S.end_capture())
                S.replay(*lists, span=[(0.0, 1.0), (0.0, 0.7)] + [(0.0, 1.0)] * (len(lists) - 2) if len(lists) >= 2 else None)
            if not full:
                return

            cur["big"] = [0, 1, 2, 3, 4, 5]
            mkeys = [("mixT", k) for k in range(KT)]
            for cb in range(4):
                s = ring_load([(w_out_v, 0, cb * 512, 512, 0)])
                for mi, (off, n) in enumerate(mtiles):
                    b = nbig()
                    gemm_tm(mixT, mkeys, s, 0, 512, off, n, big[b][:n, :], ("big", b))
                    S.op("dve", lambda v, mi=mi, n=n, cb=cb, b=b: v.tensor_add(out=hbuf[:n, mi, cb * 512:(cb + 1) * 512],
                                                                              in0=hbuf[:n, mi, cb * 512:(cb + 1) * 512], in1=big[b][:n, :]),
                         reads=[("h", mi, cb), ("big", b)], writes=[("h", mi, cb)])
                    if cb == 3:
                        norm_to_actT(mtiles, gmlpT, only=mi)
            if part_a:
                S.op("dve", lambda v: v.tensor_copy(out=actS[:, :, 0:T], in_=actT[:, :, 0:T]), reads=akeys, writes=["actS"])
                for mi, (off, n) in enumerate(mtiles):
                    S.op("sp", lambda q, mi=mi, off=off, n=n: [q.dma_start(out=hs_scr[off:off + n, :], in_=hbuf[:n, mi, :])],
                         reads=[("h", mi, cb) for cb in range(4)], writes=["hs_scr"], dma=("hss", mi))
                return
            def final_norm_store(mi, n, dst, nxt_rows=None):
                hk = [("h", mi, cb) for cb in range(4)]
                xi = nxt("xn", 2)
                sc_ = 2 * nxt("stt", 32)
                S.op("act", lambda a: a.activation(out=xn[xi][:n, :], in_=hbuf[:n, mi, :], func=AF.Square, accum_out=stt[:n, sc_:sc_ + 1]),
                     reads=hk, writes=[("xn", xi), ("stt", sc_)])
                rsqrt_stat(sc_, n, 1.0 / D)
                S.op("dve", lambda v: v.scalar_tensor_tensor(out=hbuf[:n, mi, :], in0=hbuf[:n, mi, :], scalar=stt[:n, sc_ + 1:sc_ + 2],
                                                             in1=gfin[:n, :], op0=ALU.mult, op1=ALU.mult),
                     reads=hk + [("stt", sc_ + 1), "gfin"], writes=hk)
                S.op("sp", lambda q: [q.dma_start(out=dst, in_=hbuf[:n, mi, :])], reads=hk, dma=("yo", mi))
                if nxt_rows is not None:
                    S.op("sp", lambda q: [q.dma_start(out=hbuf[:n, mi, :], in_=nxt_rows)], writes=hk, dma=("h", mi))

            if extra:
                cur["big"] = [0, 1, 2, 3]
            hkeys = [("hidS", k) for k in range(KT)]
            for qf in range(4):
                for sbk in range(4):
                    s = ring_load([(w_up_v, 0, qf * 2048 + sbk * 512, 512, 0)])
                    for j in range(4):
                        b = nbig()
                        gemm_fm(s, j * 128, 128, (0, T), big[b][:, :T], ("big", b), akeys)
                        ri = nxt("cb", 2)
                        S.op("act", lambda a, b=b, ri=ri: a.activation(out=scr[:, ri, :T], in_=big[b][:, :T], func=AF.Relu),
                             reads=[("big", b)], writes=[("scr", ri)])
                        S.op("dve", lambda v, ri=ri, sbk=sbk, j=j: v.tensor_mul(out=mixT[:, sbk * 4 + j, :T], in0=scr[:, ri, :T], in1=scr[:, ri, :T]),
                             reads=[("scr", ri)], writes=[("mixT", sbk * 4 + j)])
                        if extra:
                            b2 = nbig()
                            gemm_fm(s, j * 128, 128, (0, 32), big[b2][:, :32], ("big", b2), ["actS"], src_=actS)
                            r2 = 2 + nxt("cb2", 2)
                            S.op("act", lambda a, b2=b2, r2=r2: a.activation(out=scr[:, r2, :32], in_=big[b2][:, :32], func=AF.Relu),
                                 reads=[("big", b2)], writes=[("scr", r2)])
                            S.op("dve", lambda v, r2=r2, sbk=sbk, j=j: v.tensor_mul(out=hidS[:, sbk * 4 + j, :], in0=scr[:, r2, :32], in1=scr[:, r2, :32]),
                                 reads=[("scr", r2)], writes=[("hidS", sbk * 4 + j)])
                for cb in range(4):
                    s = ring_load([(w_down_v, qf * 16, cb * 512, 512, 0)])
                    for mi, (off, n) in enumerate(mtiles):
                        b = nbig()
                        gemm_tm(mixT, mkeys, s, 0, 512, off, n, big[b][:n, :], ("big", b))
                        S.op("dve", lambda v, mi=mi, n=n, cb=cb, b=b: v.tensor_add(out=hbuf[:n, mi, cb * 512:(cb + 1) * 512],
                                                                                  in0=hbuf[:n, mi, cb * 512:(cb + 1) * 512], in1=big[b][:n, :]),
                             reads=[("h", mi, cb), ("big", b)], writes=[("h", mi, cb)])
                        if qf == 3 and cb == 3:
                            final_norm_store(mi, n, out_y[off:off + n, :], None if next_x is None else next_x[off:off + n, :])
                    if extra:
                        def sfn(t, s=s, cb=cb, qf=qf):
                            last = None
                            for k in range(KT):
                                last = t.matmul(big[4 + cb][:32, :], lhsT=hidS[:, k, 0:32], rhs=ring[s][:, k, :],
                                                start=(qf == 0 and k == 0), stop=(qf == 3 and k == KT - 1))
                            return last
                        S.op("pe", sfn, reads=[("ring", s)] + hkeys, writes=[("big", 4 + cb)])

            if extra:
                S.op("sp", lambda q: [q.dma_start(out=hbuf[:32, 0, :], in_=hs_scr)], reads=["hs_scr"],
                     writes=[("h", 0, cb) for cb in range(4)], dma=("h", 0))
                for cb in range(4):
                    S.op("dve", lambda v, cb=cb: v.tensor_add(out=hbuf[:32, 0, cb * 512:(cb + 1) * 512], in0=hbuf[:32, 0, cb * 512:(cb + 1) * 512],
                                                             in1=big[4 + cb][:32, :]),
                         reads=[("h", 0, cb), ("big", 4 + cb)], writes=[("h", 0, cb)])
                final_norm_store(0, 32, ys)

        def store_conv_tail(nrows_src, src_lo):
            pflat = big[7]
            for ct in range(8):
                q4 = ct % 4
                S.op("pe", lambda t, ct=ct, q4=q4: t.transpose(out=pflat[:nrows_src, q4 * 128:(q4 + 1) * 128],
                                                               in_=utail[:, ct, src_lo:src_lo + nrows_src], identity=identf[:]),
                     reads=[("utail", ct), "identf"], writes=[("big", 7)])
                S.op("dve", lambda v, ct=ct, q4=q4: v.tensor_copy(out=tok1k[:nrows_src, ct * 128:(ct + 1) * 128],
                                                                 in_=pflat[:nrows_src, q4 * 128:(q4 + 1) * 128]),
                     reads=[("big", 7)], writes=[("tok1k", ct), ("xn", 1)])

        pm = [(i * 128, 128) for i in range(4)]
        if sample:
            for j in range(2):
                S.op("sp", lambda q, j=j: [q.dma_start(out=Sst[j][:], in_=sh[j].rearrange("h d e -> d h e"))],
                     writes=[("S", j, h) for h in range(NH)], dma=("shl", j))
                S.op("dve", lambda v, j=j: v.tensor_copy(out=Sbf[j][:], in_=Sst[j][:]), reads=[("S", j, h) for h in range(NH)],
                     writes=[("Sbf", j, h) for h in range(NH)])
                S.op("sp", lambda q, j=j: [q.dma_start(out=tok1k[0:30, :], in_=sc[j])], writes=[("tok1k", ct) for ct in range(8)] + [("xn", 1)], dma="scl")
                for ct in range(8):
                    S.op("pe", lambda t, ct=ct: t.transpose(out=big[6][:, ct * 32:ct * 32 + 30], in_=tok1k[0:30, ct * 128:(ct + 1) * 128], identity=identf[0:30, 0:30]),
                         reads=[("tok1k", ct), ("xn", 1), "identf"], writes=[("big", 6)])
                    S.op("dve", lambda v, ct=ct, j=j: v.tensor_copy(out=ubs[:, ct, j, 0:30], in_=big[6][:, ct * 32:ct * 32 + 30]),
                         reads=[("big", 6)], writes=[("us", ct, j)])
                S.op("sp", lambda q, j=j: [q.dma_start(out=ncs[j, 0:14, :], in_=sc[j, 16:30, :])], dma=("ncs0", j))
            process_tb("sample", xs, 32, [(0, 16), (16, 16)], [0, 1], part_a=True)
            for j in range(2):
                S.op("sp", lambda q, j=j: [q.dma_start(out=nhs[j].rearrange("h d e -> d h e"), in_=Sst[j][:])],
                     reads=[("S", j, h) for h in range(NH)], dma=("nhs", j))
                store_conv_tail(16, j * 16)
                S.op("sp", lambda q, j=j: [q.dma_start(out=ncs[j, 14:30, :], in_=tok1k[0:16, :])], reads=[("tok1k", ct) for ct in range(8)] + [("xn", 1)],
                     dma=("ncs1", j))
        S.op("dve", lambda v: v.memset(Sst[0][:], 0.0), writes=[("S", 0, h) for h in range(NH)])
        S.op("dve", lambda v: v.memset(Sbf[0][:], 0.0), writes=[("Sbf", 0, h) for h in range(NH)])
        for tb in range(n_prev):
            process_tb("prev_last" if tb == n_prev - 1 else "prev", xp[tb * TW:(tb + 1) * TW, :], TW, pm, [0, 0, 0, 0])
        for tb in range(n_main):
            last = tb == n_main - 1
            process_tb("main", xm[tb * TW:(tb + 1) * TW, :], TW, pm, [0, 0, 0, 0], out_y=yp[tb * TW:(tb + 1) * TW, :],
                       last_main=last, extra=(last and sample),
                       next_x=None if last else xm[(tb + 1) * TW:(tb + 2) * TW, :], preloaded=(tb > 0))
        S.op("sp", lambda q: [q.dma_start(out=nhp.rearrange("h d e -> d h e"), in_=Sst[0][:])], reads=[("S", 0, h) for h in range(NH)], dma="nhp")
        store_conv_tail(32, 0)
        S.op("sp", lambda q: [q.dma_start(out=ncp[:, :], in_=tok1k[2:32, :])], reads=[("tok1k", ct) for ct in range(8)] + [("xn", 1)], dma="ncp")
        S.emit()
    return nc


def _pack_pvec(norm_mix_g, norm_mlp_g, b_dw, ln_g, ln_b, lb_logits, w_dw):
    pv = np.zeros((128, NPV), np.float32)
    pv[:, 0:16] = norm_mix_g.reshape(16, 128).T
    pv[:, 16:32] = norm_mlp_g.reshape(16, 128).T
    pv[:, 32:40] = b_dw.reshape(8, 128).T
    pv[:, 40:48] = ln_g.reshape(8, 128).T
    pv[:, 48:56] = ln_b.reshape(8, 128).T
    pv[:, 56:64] = lb_logits[0].reshape(8, 128).T
    pv[:, 64:72] = lb_logits[1].reshape(8, 128).T
    pv[:, 72:72 + 248] = w_dw.T.reshape(8, 128, CW).transpose(1, 0, 2).reshape(128, 248)
    return pv


_NC_CACHE = {}


def kernel(x_prompt, x_sample, state_conv, state_hgrn, norm_mix_g, w_in, w_dw, b_dw, ln_conv_g, ln_conv_b,
           lb_logits, hgrn_norm_g, w_out, norm_mlp_g, w_up, w_down, norm_final_g):
    f = lambda a: np.ascontiguousarray(np.asarray(a, dtype=np.float32))
    x_prompt, x_sample, state_conv, state_hgrn = f(x_prompt), f(x_sample), f(state_conv), f(state_hgrn)
    B, SEQ, _ = x_prompt.shape
    half = SEQ // 2
    pv = _pack_pvec(f(norm_mix_g)[0], f(norm_mlp_g)[0], f(b_dw)[0], f(ln_conv_g)[0], f(ln_conv_b)[0], f(lb_logits), f(w_dw)[0])
    w_in0 = f(w_in)[0]
    cols = []
    for p in range(4):
        cols += list(range(p * 256, (p + 1) * 256)) + list(range(1024 + p * 256, 1024 + (p + 1) * 256))
    for h in range(NH):
        for i in range(4):
            cols += list(range(2048 + i * 1024 + h * 128, 2048 + i * 1024 + (h + 1) * 128))
    w_in_hm = np.ascontiguousarray(w_in0[:, np.asarray(cols)])
    shared = {"w_in": w_in0, "w_in_hm": w_in_hm, "w_out": f(w_out)[0], "w_up": f(w_up)[0], "w_down": f(w_down)[0], "pvec": pv,
              "gfin": f(norm_final_g), "hg": f(hgrn_norm_g)[0]}
    zeros_half = np.zeros((half, D), np.float32)
    in_maps = []
    for c in range(8):
        s, hf = c // 2, c % 2
        m = dict(shared)
        m["xm"] = np.ascontiguousarray(x_prompt[s, hf * half:(hf + 1) * half])
        m["xp"] = np.ascontiguousarray(x_prompt[s, 0:half]) if hf == 1 else zeros_half
        m["xs"] = np.ascontiguousarray(x_sample[2 * c:2 * c + 2].reshape(32, D))
        m["sc"] = np.ascontiguousarray(state_conv[0, 2 * c:2 * c + 2])
        m["sh"] = np.ascontiguousarray(state_hgrn[0, 2 * c:2 * c + 2])
        in_maps.append(m)
    if "nc" not in _NC_CACHE:
        _NC_CACHE["nc"] = build()
    res = run_bass_kernel_spmd(_NC_CACHE["nc"], in_maps, core_ids=list(range(8)))
    r = res.results
    y_prompt = np.zeros((B, SEQ, D), np.float32)
    y_sample = np.zeros((16, 16, D), np.float32)
    ncp = np.zeros((1, B, 30, CONV), np.float32)
    nhp = np.zeros((1, B, NH, 128, 128), np.float32)
    ncs = np.zeros((1, 16, 30, CONV), np.float32)
    nhs = np.zeros((1, 16, NH, 128, 128), np.float32)
    for c in range(8):
        s, hf = c // 2, c % 2
        y_prompt[s, hf * half:(hf + 1) * half] = r[c]["yp"]
        y_sample[2 * c:2 * c + 2] = r[c]["ys"].reshape(2, 16, D)
        ncs[0, 2 * c:2 * c + 2] = r[c]["ncs"]
        nhs[0, 2 * c:2 * c + 2] = r[c]["nhs"]
        if hf == 1:
            ncp[0, s] = r[c]["ncp"]
            nhp[0, s] = r[c]["nhp"]
    return (y_prompt, y_sample, ncp, nhp, ncs, nhs)
```
